# Optimizing a Trainium2 kernel written in Bass

```python
import functools
import jax, jax.numpy as jnp
from jax import lax
import numpy as np

D_MODEL = 2048
BATCH = 4
SEQ = 2048
DEPTH = 1
DEC_BATCH = 128
DEC_SEQ = 8
PAST_LEN = 16384
PAGE_SIZE = 128

DN_HEADS = 8
DN_DK = 128
DN_DV = 128
DN_CONV = 4
DN_QK = DN_HEADS * DN_DK
DN_VW = DN_HEADS * DN_DV
DN_QKV = 2 * DN_QK + DN_VW
ML_HEADS = 4
ML_DK = 128
ML_DV = 256
ML_QK = ML_HEADS * ML_DK
ML_VW = ML_HEADS * ML_DV
MIX_WIDTH = DN_VW + ML_VW
D_FF = 5632
FFN_CONV = 3
CHUNK = 64
EPS = 1e-6
PROJ_SIZES = (DN_QKV, DN_VW, DN_HEADS, DN_HEADS, ML_QK, ML_QK, ML_VW, ML_VW, ML_HEADS, ML_HEADS)
PROJ_COLS = DN_QKV + DN_VW + 2 * DN_HEADS + 2 * ML_QK + 2 * ML_VW + 2 * ML_HEADS

kernel_name = "hymba_gdn_mlstm_convffn_step"


def _rmsnorm(x, g):
    xf = x.astype(jnp.float32)
    y = xf * lax.rsqrt(jnp.mean(xf * xf, axis=-1, keepdims=True) + EPS)
    return (y * g.astype(jnp.float32)).astype(x.dtype)


def _l2norm(x):
    return x * lax.rsqrt(jnp.sum(x * x, axis=-1, keepdims=True) + EPS)


def _causal_dwconv(x, buf, w):
    width, L = w.shape[0], x.shape[1]
    xp = jnp.concatenate([buf.astype(jnp.float32), x.astype(jnp.float32)], axis=1)
    wf = w.astype(jnp.float32)
    out = sum(xp[:, j:j + L] * wf[j] for j in range(width))
    return out, xp[:, L:]


def _chunk_size(L):
    return CHUNK if L % CHUNK == 0 else L


def _to_chunks(t, n, c):
    t = t.reshape((t.shape[0], n, c) + t.shape[2:])
    return jnp.swapaxes(jnp.swapaxes(t, 0, 1), 2, 3)


def _from_chunks(o):
    n, b, h, c, d = o.shape
    return o.transpose(1, 0, 3, 2, 4).reshape(b, n * c, h, d)


def _gated_delta_rule(q, k, v, g, beta, S0):
    L = q.shape[1]
    c = _chunk_size(L)
    n = L // c
    q, k, v, g, beta = (_to_chunks(t, n, c) for t in (q, k, v, g, beta))
    G = jnp.cumsum(g, axis=-1)
    tril = jnp.tril(jnp.ones((c, c), dtype=bool))
    strict = jnp.tril(jnp.ones((c, c), dtype=bool), -1)
    decay = jnp.exp(jnp.where(tril, G[..., :, None] - G[..., None, :], -jnp.inf))
    kb = k * beta[..., None]
    lower = jnp.where(strict, jnp.einsum('nbhik,nbhjk->nbhij', kb, k) * decay, 0.0)
    a_mat = lower + jnp.eye(c, dtype=lower.dtype)
    solve = functools.partial(lax.linalg.triangular_solve, left_side=True, lower=True, unit_diagonal=True)
    u = solve(a_mat, v * beta[..., None])
    w = solve(a_mat, kb * jnp.exp(G)[..., None])
    qk = jnp.einsum('nbhik,nbhjk->nbhij', q, k) * decay
    qe = q * jnp.exp(G)[..., None]
    G_last = G[..., -1:]
    kd = k * jnp.exp(G_last - G)[..., None]
    gl = jnp.exp(G_last[..., 0])

    def step(S, xs):
        qe_c, qk_c, u_c, w_c, kd_c, gl_c = xs
        v_new = u_c - jnp.einsum('bhck,bhkv->bhcv', w_c, S)
        o = jnp.einsum('bhck,bhkv->bhcv', qe_c, S) + jnp.einsum('bhij,bhjv->bhiv', qk_c, v_new)
        S = S * gl_c[..., None, None] + jnp.einsum('bhck,bhcv->bhkv', kd_c, v_new)
        return S, o

    S, o = lax.scan(step, S0, (qe, qk, u, w, kd, gl))
    return _from_chunks(o), S


def _mlstm(q, k, v, ig, logf, C0, n0, m0):
    L = q.shape[1]
    c = _chunk_size(L)
    n = L // c
    q, k, v, ig, logf = (_to_chunks(t, n, c) for t in (q, k, v, ig, logf))
    tril = jnp.tril(jnp.ones((c, c), dtype=bool))

    def step(carry, xs):
        Cm, nv, m = carry
        q_c, k_c, v_c, i_c, f_c = xs
        b = jnp.cumsum(f_c, axis=-1)
        D = jnp.where(tril, b[..., :, None] - b[..., None, :] + i_c[..., None, :], -jnp.inf)
        m_new = jnp.maximum(b + m[..., None], jnp.max(D, axis=-1))
        inter = jnp.exp(b + m[..., None] - m_new)
        s = jnp.einsum('bhik,bhjk->bhij', q_c, k_c) * jnp.exp(D - m_new[..., None])
        num = inter[..., None] * jnp.einsum('bhck,bhkv->bhcv', q_c, Cm) + jnp.einsum('bhij,bhjv->bhiv', s, v_c)
        den = inter * jnp.einsum('bhck,bhk->bhc', q_c, nv) + jnp.sum(s, axis=-1)
        h = num / jnp.maximum(jnp.abs(den), jnp.exp(-m_new))[..., None]
        m_end = m_new[..., -1]
        w_end = jnp.exp(b[..., -1:] - b + i_c - m_end[..., None])
        carry_scale = jnp.exp(b[..., -1] + m - m_end)
        Cm = carry_scale[..., None, None] * Cm + jnp.einsum('bhck,bhcv->bhkv', k_c * w_end[..., None], v_c)
        nv = carry_scale[..., None] * nv + jnp.einsum('bhck,bhc->bhk', k_c, w_end)
        return (Cm, nv, m_end), h

    (Cm, nv, m), h = lax.scan(step, (C0, n0, m0), (q, k, v, ig, logf))
    return _from_chunks(h), Cm, nv, m


def _layer(x, st, p):
    conv_buf, S0, C0, n0, m0, ffn_buf = st
    (norm_mix_g, w_in, dn_conv_w, dn_A_log, dn_dt_bias, dn_norm_g, ml_i_bias, ml_f_bias,
     ml_norm_g, w_out, norm_ffn_g, w_up, ffn_conv_w, ffn_conv_b, w_down) = p
    f32 = jnp.float32
    B, L, _ = x.shape
    h = _rmsnorm(x, norm_mix_g)
    proj = jnp.einsum('bld,de->ble', h, w_in)
    idx = np.cumsum(PROJ_SIZES)[:-1].tolist()
    dn_qkv, dn_z, dn_b, dn_a, ml_q, ml_k, ml_v, ml_o, ml_i, ml_f = jnp.split(proj, idx, axis=-1)

    qkv_c, conv_new = _causal_dwconv(dn_qkv, conv_buf, dn_conv_w)
    qkv_c = jax.nn.silu(qkv_c)
    dq, dkk, dvv = jnp.split(qkv_c, [DN_QK, 2 * DN_QK], axis=-1)
    dq = _l2norm(dq.reshape(B, L, DN_HEADS, DN_DK)) * (DN_DK ** -0.5)
    dkk = _l2norm(dkk.reshape(B, L, DN_HEADS, DN_DK))
    dvv = dvv.reshape(B, L, DN_HEADS, DN_DV)
    beta = jax.nn.sigmoid(dn_b.astype(f32))
    g = -jnp.exp(dn_A_log.astype(f32)) * jax.nn.softplus(dn_a.astype(f32) + dn_dt_bias.astype(f32))
    o_dn, S_new = _gated_delta_rule(dq, dkk, dvv, g, beta, S0.astype(f32))
    o_dn = _rmsnorm(o_dn, dn_norm_g) * jax.nn.silu(dn_z.astype(f32).reshape(B, L, DN_HEADS, DN_DV))

    mq = ml_q.astype(f32).reshape(B, L, ML_HEADS, ML_DK) * (ML_DK ** -0.5)
    mk = ml_k.astype(f32).reshape(B, L, ML_HEADS, ML_DK)
    mv = ml_v.astype(f32).reshape(B, L, ML_HEADS, ML_DV)
    ig = ml_i.astype(f32) + ml_i_bias.astype(f32)
    logf = jax.nn.log_sigmoid(ml_f.astype(f32) + ml_f_bias.astype(f32))
    h_ml, C_new, n_new, m_new = _mlstm(mq, mk, mv, ig, logf, C0.astype(f32), n0.astype(f32), m0.astype(f32))
    o_ml = _rmsnorm(h_ml, ml_norm_g.reshape(ML_HEADS, ML_DV)) * jax.nn.sigmoid(ml_o.astype(f32).reshape(B, L, ML_HEADS, ML_DV))

    mix = jnp.concatenate([o_dn.reshape(B, L, DN_VW), o_ml.reshape(B, L, ML_VW)], axis=-1).astype(x.dtype)
    x = x + jnp.einsum('ble,ed->bld', mix, w_out)

    h = _rmsnorm(x, norm_ffn_g)
    u = jnp.einsum('bld,df->blf', h, w_up)
    uc, ffn_new = _causal_dwconv(u, ffn_buf, ffn_conv_w)
    uc = uc + ffn_conv_b.astype(f32)
    gate, up = jnp.split(uc, 2, axis=-1)
    x = x + jnp.einsum('blf,fd->bld', (jax.nn.silu(gate) * up).astype(x.dtype), w_down)
    new_st = (conv_new.astype(conv_buf.dtype), S_new.astype(S0.dtype), C_new.astype(C0.dtype),
              n_new.astype(n0.dtype), m_new.astype(m0.dtype), ffn_new.astype(ffn_buf.dtype))
    return x, new_st


def setup_inputs(seed: int = 0) -> dict:
    key = jax.random.key(seed)
    ks = jax.random.split(key, 32)
    nrm = lambda k, s, sc=1.0: jax.random.normal(k, s, jnp.float32) * sc
    dt = jnp.exp(jax.random.uniform(ks[10], (DEPTH, DN_HEADS), jnp.float32, np.log(1e-3), np.log(1e-1)))
    return {
        "x_prompt": nrm(ks[0], (BATCH, SEQ, D_MODEL)),
        "x_sample": nrm(ks[1], (DEC_BATCH, DEC_SEQ, D_MODEL)),
        "state_dn_conv": nrm(ks[2], (DEPTH, DEC_BATCH, DN_CONV - 1, DN_QKV)),
        "state_dn_S": nrm(ks[3], (DEPTH, DEC_BATCH, DN_HEADS, DN_DK, DN_DV), 0.1),
        "state_ml_C": nrm(ks[4], (DEPTH, DEC_BATCH, ML_HEADS, ML_DK, ML_DV), 0.1),
        "state_ml_n": jnp.abs(nrm(ks[5], (DEPTH, DEC_BATCH, ML_HEADS, ML_DK), 0.1)),
        "state_ml_m": nrm(ks[6], (DEPTH, DEC_BATCH, ML_HEADS)),
        "state_ffn_conv": nrm(ks[7], (DEPTH, DEC_BATCH, FFN_CONV - 1, 2 * D_FF)),
        "norm_mix_g": 1.0 + nrm(ks[8], (DEPTH, D_MODEL), 0.02),
        "w_in": nrm(ks[9], (DEPTH, D_MODEL, PROJ_COLS), D_MODEL ** -0.5),
        "dn_conv_w": nrm(ks[11], (DEPTH, DN_CONV, DN_QKV), DN_CONV ** -0.5),
        "dn_A_log": jnp.log(jax.random.uniform(ks[12], (DEPTH, DN_HEADS), jnp.float32, 1.0, 16.0)),
        "dn_dt_bias": dt + jnp.log(-jnp.expm1(-dt)),
        "dn_norm_g": 1.0 + nrm(ks[13], (DEPTH, DN_DV), 0.02),
        "ml_i_bias": -3.0 + nrm(ks[14], (DEPTH, ML_HEADS), 0.1),
        "ml_f_bias": jnp.linspace(3.0, 6.0, ML_HEADS, dtype=jnp.float32) + nrm(ks[15], (DEPTH, ML_HEADS), 0.1),
        "ml_norm_g": 1.0 + nrm(ks[16], (DEPTH, ML_VW), 0.02),
        "w_out": nrm(ks[17], (DEPTH, MIX_WIDTH, D_MODEL), MIX_WIDTH ** -0.5),
        "norm_ffn_g": 1.0 + nrm(ks[18], (DEPTH, D_MODEL), 0.02),
        "w_up": nrm(ks[19], (DEPTH, D_MODEL, 2 * D_FF), D_MODEL ** -0.5),
        "ffn_conv_w": nrm(ks[20], (DEPTH, FFN_CONV, 2 * D_FF), FFN_CONV ** -0.5),
        "ffn_conv_b": nrm(ks[21], (DEPTH, 2 * D_FF), 0.02),
        "w_down": nrm(ks[22], (DEPTH, D_FF, D_MODEL), D_FF ** -0.5),
        "norm_final_g": 1.0 + nrm(ks[23], (D_MODEL,), 0.02),
    }


def reference(x_prompt, x_sample, state_dn_conv, state_dn_S, state_ml_C, state_ml_n, state_ml_m,
              state_ffn_conv, norm_mix_g, w_in, dn_conv_w, dn_A_log, dn_dt_bias, dn_norm_g,
              ml_i_bias, ml_f_bias, ml_norm_g, w_out, norm_ffn_g, w_up, ffn_conv_w, ffn_conv_b,
              w_down, norm_final_g):
    weights = (norm_mix_g, w_in, dn_conv_w, dn_A_log, dn_dt_bias, dn_norm_g, ml_i_bias, ml_f_bias,
               ml_norm_g, w_out, norm_ffn_g, w_up, ffn_conv_w, ffn_conv_b, w_down)
    states = (state_dn_conv, state_dn_S, state_ml_C, state_ml_n, state_ml_m, state_ffn_conv)
    xp, xs = x_prompt, x_sample
    p_new = [[] for _ in states]
    s_new = [[] for _ in states]
    for l in range(DEPTH):
        p = tuple(w[l] for w in weights)
        st_p = tuple(jnp.zeros((BATCH,) + s.shape[2:], s.dtype) for s in states)
        st_s = tuple(s[l] for s in states)
        xp, np_st = _layer(xp, st_p, p)
        xs, ns_st = _layer(xs, st_s, p)
        for i in range(len(states)):
            p_new[i].append(np_st[i])
            s_new[i].append(ns_st[i])
    y_prompt = _rmsnorm(xp, norm_final_g)
    y_sample = _rmsnorm(xs, norm_final_g)
    p_dn_conv, p_dn_S, p_ml_C, p_ml_n, p_ml_m, p_ffn_conv = (jnp.stack(a, axis=0) for a in p_new)
    s_dn_conv, s_dn_S, s_ml_C, s_ml_n, s_ml_m, s_ffn_conv = (jnp.stack(a, axis=0) for a in s_new)
    return (y_prompt, y_sample, p_dn_conv, p_dn_S, p_ml_C, p_ml_n, p_ml_m, p_ffn_conv,
            s_dn_conv, s_dn_S, s_ml_C, s_ml_n, s_ml_m, s_ffn_conv)
```

```python
import numpy as np
from contextlib import ExitStack
import concourse.bass as bass
import concourse.mybir as mybir
from concourse.bass_utils import run_bass_kernel_spmd

F32 = mybir.dt.float32
BF16 = mybir.dt.bfloat16
AF = mybir.ActivationFunctionType
ALU = mybir.AluOpType
AX = mybir.AxisListType

D = 2048
NPT = 16
EPS = 1e-6
NEG = -30000.0
SAME_ENG_SYNC = True
C_DNQ, C_DNK, C_DNV, C_DNZ, C_DNB, C_DNA = 0, 1024, 2048, 3072, 4096, 4104
C_MLQ, C_MLK, C_MLV, C_MLO, C_MLI, C_MLF = 4112, 4624, 5136, 6160, 7184, 7188
DFF = 5632


class Dep:
    __slots__ = ("w", "r", "sem", "cnt", "key", "excl")

    def __init__(self):
        self.excl = False
        self.w = None
        self.r = {}
        self.sem = None
        self.cnt = 0
        self.key = None


class V:
    def __init__(self, ap, deps):
        self.ap = ap
        self.deps = deps

    def __getitem__(self, idx):
        return V(self.ap[idx], self.deps)


class K:
    def __init__(self, nc, es):
        self.nc = nc
        self.es = es
        self.eng = {"pe": nc.tensor, "dve": nc.vector, "act": nc.scalar, "pool": nc.gpsimd, "sp": nc.sync}
        self.sem = {e: es.enter_context(nc.semaphore("s_" + e)) for e in ("pe", "dve", "act", "pool")}
        self.cnt = {e: 0 for e in self.sem}
        self.waited = {e: {} for e in self.eng}
        self.pending = {e: [] for e in self.sem}
        self.nsem = 0
        self.dma_deps = []
        self.nalloc = 0
        self.barrier = {}

    def newdep(self):
        d = Dep()
        d.r = dict(self.barrier)
        return d

    def fence(self):
        for e in self.sem:
            if self.cnt[e] > 0:
                self.barrier[e] = (e, self.sem[e], self.cnt[e])
        for sd in self.dma_deps:
            self.barrier[sd.key] = (sd.key, sd.sem, sd.cnt)

    def sb(self, shape, dt=F32, es=None):
        self.nalloc += 1
        t = (es or self.es).enter_context(self.nc.sbuf_tensor("t%d" % self.nalloc, list(shape), dt))
        return V(t[:], [self.newdep()])

    def psb(self, shape, dt=F32, es=None):
        self.nalloc += 1
        t = (es or self.es).enter_context(self.nc.psum_tensor("p%d" % self.nalloc, list(shape), dt))
        return t

    def view(self, v, idx):
        return V(v.ap[idx], [self.newdep()])

    def _wait(self, E, ev):
        if ev is None:
            return
        key, sem, val = ev
        if key == E and (E == "pe" or not SAME_ENG_SYNC):
            return
        if self.waited[E].get(key, 0) >= val:
            return
        self.eng[E].wait_ge(sem, val)
        self.waited[E][key] = val

    def _pre(self, E, R, W):
        for d in R:
            self._wait(E, d.w)
            if d.excl:
                for ek, ev in list(d.r.items()):
                    if ek != E:
                        self._wait(E, ev)
        for d in W:
            self._wait(E, d.w)
            for ev in list(d.r.values()):
                self._wait(E, ev)

    def op(self, E, fn, outs, ins, inc=True):
        R = [d for v in ins if isinstance(v, V) for d in v.deps]
        W = [d for v in outs if isinstance(v, V) for d in v.deps]
        self._pre(E, R, W)
        ins_ = fn()
        if not inc:
            self.pending[E].append((R, W))
            return
        self.cnt[E] += 1
        ins_.then_inc(self.sem[E], 1)
        ev = (E, self.sem[E], self.cnt[E])
        for (R2, W2) in self.pending[E] + [(R, W)]:
            for d in R2:
                d.r[E] = ev
            for d in W2:
                d.w = ev
                d.r = {}
        self.pending[E] = []

    def dma(self, Q, out, in_, **kw):
        R = in_.deps if isinstance(in_, V) else []
        W = out.deps if isinstance(out, V) else []
        self._pre(Q, R, W)
        sd = (W or R)[0]
        if sd.sem is None:
            self.nsem += 1
            sd.sem = self.es.enter_context(self.nc.semaphore("d%d" % self.nsem))
            sd.key = "d%d" % self.nsem
            self.dma_deps.append(sd)
        o = out.ap if isinstance(out, V) else out
        i = in_.ap if isinstance(in_, V) else in_
        self.eng[Q].dma_start(out=o, in_=i, **kw).then_inc(sd.sem, 16)
        sd.cnt += 16
        ev = (sd.key, sd.sem, sd.cnt)
        for d in R:
            d.r[sd.key] = ev
        for d in W:
            d.w = ev
            d.r = {}

    def finish(self):
        for sd in self.dma_deps:
            self.nc.sync.wait_ge(sd.sem, sd.cnt)

    @staticmethod
    def _a(x):
        return x.ap if isinstance(x, V) else x

    def mm(self, out, lhsT, rhs, start=True, stop=True):
        self.op("pe", lambda: self.nc.tensor.matmul(out.ap, lhsT=lhsT.ap, rhs=rhs.ap, start=start, stop=stop),
                [out], [lhsT, rhs], inc=stop)

    def tr(self, out, in_, ident):
        if in_.ap.dtype == F32:
            self.op("pe", lambda: self.nc.tensor.matmul(out.ap, lhsT=in_.ap, rhs=ident.ap, start=True, stop=True), [out], [in_, ident])
        else:
            self.op("pe", lambda: self.nc.tensor.transpose(out=out.ap, in_=in_.ap, identity=ident.ap), [out], [in_, ident])

    def act(self, out, in_, func, bias=None, scale=None, accum=None):
        kw = {}
        if bias is not None:
            kw["bias"] = self._a(bias)
        if scale is not None:
            kw["scale"] = self._a(scale)
        if accum is not None:
            kw["accum_out"] = accum.ap
        outs = [out] + ([accum] if accum is not None else [])
        self.op("act", lambda: self.nc.scalar.activation(out=out.ap, in_=in_.ap, func=func, **kw), outs, [in_, bias, scale])

    def ts(self, E, out, in0, s1, s2, op0, op1=None):
        e = self.eng[E]
        if op1 is None:
            self.op(E, lambda: e.tensor_scalar(out=out.ap, in0=in0.ap, scalar1=self._a(s1), scalar2=None, op0=op0), [out], [in0, s1])
        else:
            self.op(E, lambda: e.tensor_scalar(out=out.ap, in0=in0.ap, scalar1=self._a(s1), scalar2=self._a(s2), op0=op0, op1=op1),
                    [out], [in0, s1, s2])

    def stt(self, E, out, in0, scalar, in1, op0, op1):
        e = self.eng[E]
        self.op(E, lambda: e.scalar_tensor_tensor(out=out.ap, in0=in0.ap, scalar=self._a(scalar), in1=in1.ap, op0=op0, op1=op1),
                [out], [in0, scalar, in1])

    def tt(self, E, out, in0, in1, op):
        e = self.eng[E]
        self.op(E, lambda: e.tensor_tensor(out=out.ap, in0=in0.ap, in1=in1.ap, op=op), [out], [in0, in1])

    def cp(self, E, out, in_):
        if E == "act":
            self.act(out, in_, AF.Copy)
        else:
            e = self.eng[E]
            self.op(E, lambda: e.tensor_copy(out=out.ap, in_=in_.ap), [out], [in_])

    def redmax(self, out, in_):
        self.op("dve", lambda: self.nc.vector.reduce_max(out=out.ap, in_=in_.ap, axis=AX.X), [out], [in_])

    def recip(self, out, in_):
        self.op("dve", lambda: self.nc.vector.reciprocal(out=out.ap, in_=in_.ap), [out], [in_])

    def memset(self, E, out, val):
        e = self.eng[E]
        self.op(E, lambda: e.memset(out.ap, val), [out], [])


def bc(v, shape):
    return V(v.ap.to_broadcast(list(shape)), v.deps)


DBG = {}


def build_program():
    nc = bass.Bass("TRN2", target_bir_lowering=False)

    def din(name, shape, dt=F32):
        return nc.dram_tensor(name, list(shape), dt, kind="ExternalInput").ap()

    def dout(name, shape):
        return nc.dram_tensor(name, list(shape), F32, kind="ExternalOutput").ap()

    xp = din("xp", [2048, D]); xs = din("xs", [128, D]); xo = din("xo", [1152, D]); xh = din("xh", [2, D])
    sel = din("sel", [128, 2]); cst = din("cst", [128, 20, 128])
    sdc = din("sdc", [16, 3, 3072]); sS = din("sS", [16, 8, 128, 128]); sC = din("sC", [16, 4, 128, 256])
    sn = din("sn", [16, 4, 128]); sm = din("sm", [16, 4]); sfc = din("sfc", [16, 2, 2 * DFF])
    norm_mix_g = din("norm_mix_g", [D]); w_in = din("w_in", [D, 7192]); dn_conv_w = din("dn_conv_w", [4, 3072])
    dn_A_log = din("dn_A_log", [8]); dn_dt_bias = din("dn_dt_bias", [8]); dn_norm_g = din("dn_norm_g", [128])
    ml_i_bias = din("ml_i_bias", [4]); ml_f_bias = din("ml_f_bias", [4]); ml_norm_g = din("ml_norm_g", [1024])
    w_out = din("w_out", [D, D]); norm_ffn_g = din("norm_ffn_g", [D]); w_up = din("w_up", [D, 2 * DFF])
    ffn_conv_w = din("ffn_conv_w", [3, 2 * DFF]); ffn_conv_b = din("ffn_conv_b", [2 * DFF]); w_down = din("w_down", [DFF, D])
    norm_final_g = din("norm_final_g", [D])

    yo = dout("yo", [1152, D])
    o_pdc = dout("o_pdc", [3, 3072]); o_pS = dout("o_pS", [8, 128, 128]); o_pC = dout("o_pC", [4, 128, 256])
    o_pn = dout("o_pn", [4, 128]); o_pm = dout("o_pm", [1, 4]); o_pfc = dout("o_pfc", [2, 2 * DFF])
    o_sdc = dout("o_sdc", [16, 3, 3072]); o_sS = dout("o_sS", [16, 8, 128, 128]); o_sC = dout("o_sC", [16, 4, 128, 256])
    o_sn = dout("o_sn", [16, 4, 128]); o_sm = dout("o_sm", [16, 4]); o_sfc = dout("o_sfc", [16, 2, 2 * DFF])

    with ExitStack() as es:
        es.enter_context(nc.allow_non_contiguous_dma(reason="small strided param/state transfers"))
        k = K(nc, es)
        banks = [k.psb([128, 512]) for _ in range(8)]
        bdep = [Dep() for _ in range(8)]
        for d_ in bdep:
            d_.excl = True
        bq = [[V(banks[b][:, q * 128:(q + 1) * 128], [bdep[b]]) for q in range(4)] for b in range(8)]

        def brange(b, c0, c1):
            return V(banks[b][:, c0:c1], [bdep[b]])

        def ptb(b):
            return V(banks[b][:, :].bitcast(BF16).rearrange("p (a c) -> p a c", c=128), [bdep[b]])
        PTB = [ptb(0), ptb(1)]
        ident = k.sb([128, 128])
        k.dma("sp", ident, cst[:, 0, :])
        identb = k.sb([128, 128], BF16)
        k.cp("dve", identb, ident)
        SELT = k.sb([128, 2]); k.dma("sp", SELT, sel[:, :])
        f0 = SELT[:, 0:1]; f1 = SELT[:, 1:2]
        cc = k.sb([128, 4])
        k.memset("dve", cc[:, 0:1], EPS); k.memset("dve", cc[:, 1:2], 1.0); k.memset("dve", cc[:, 2:3], 0.0)
        epsc = cc[:, 0:1]; onec = cc[:, 1:2]

        def pload_bc(src, n):
            t = k.sb([128, n])
            k.dma("sp", t, src.partition_broadcast(128))
            return t

        def pload_fm(src, nt):
            t = k.sb([128, nt])
            k.dma("sp", t, src.rearrange("(t p) -> p t", p=128))
            return t
        gmix = pload_fm(norm_mix_g, 16); gffn = pload_fm(norm_ffn_g, 16)
        gdn = pload_fm(dn_norm_g, 1); gml = pload_fm(ml_norm_g, 8)
        Alog = pload_bc(dn_A_log, 8); dtb = pload_bc(dn_dt_bias, 8); ib = pload_bc(ml_i_bias, 4); fb = pload_bc(ml_f_bias, 4)
        negA = k.sb([128, 8])
        k.act(negA, Alog, AF.Exp)
        k.ts("dve", negA, negA, -1.0, None, ALU.mult)
        dcw = k.sb([128, 24, 4])
        for j in range(4):
            k.dma("sp", dcw[:, :, j], dn_conv_w[j, :].rearrange("(t p) -> p t", p=128))

        mixT = k.sb([128, 16, 1154], BF16)

        def rstd_from_ss(out, ss, n):
            k.act(out, ss, AF.Ln, bias=V(epsc.ap[0:ss.ap.shape[0], :], epsc.deps), scale=1.0 / n)
            k.act(out, out, AF.Exp, scale=-0.5)

        def norm_to_T(xt, np_, gfm, dstT, c0, xnb, ssq, rs):
            k.act(xnb[0:np_, :], xt[0:np_, :], AF.Square, accum=ssq[0:np_, :])
            rstd_from_ss(rs[0:np_, :], ssq[0:np_, :], D)
            k.act(xnb[0:np_, :], xt[0:np_, :], AF.Copy, scale=rs[0:np_, :])
            for hf in range(2):
                pT = PTB[hf]
                for kk in range(8):
                    kt = hf * 8 + kk
                    k.tr(pT[:, kk, 0:np_], xnb[0:np_, kt * 128:(kt + 1) * 128], identb[0:np_, 0:np_])
                gsl = V(gfm.ap[:, hf * 8:(hf + 1) * 8].unsqueeze(2), gfm.deps)
                k.tt("dve", dstT[:, hf * 8:(hf + 1) * 8, c0:c0 + np_], pT[:, 0:8, 0:np_], bc(gsl, [128, 8, np_]), ALU.mult)

        CSP = k.sb([128, 24, 3])
        k.memset("dve", CSP, 0.0)
        eA = ExitStack()
        CST = k.sb([128, 20, 128], F32, eA)
        k.dma("sp", CST, cst[:, :, :])
        ones = CST[:, 1, :]

        class Masks:
            pass
        MS = {}
        for mode, o in (("P", 2), ("S", 10)):
            m = Masks()
            m.U, m.SL, m.BO, m.LAST = (CST[:, o + i, :] for i in range(4))
            m.NEGT_incl, m.NEGT_strict, m.NEG_strict, m.NEG_incl = (CST[:, o + 4 + i, :] for i in range(4))
            MS[mode] = m
        rowmask = CST[:, 18, 0:16]; firstsel = CST[:, 18, 16:32]; seqsel = CST[0:16, 19, :]

        def run_mixer(mode):
            nt = NPT if mode == "P" else 1
            T = nt * 128
            M = MS[mode]
            nlev = 6 if mode == "P" else 2
            with ExitStack() as e1:
                hT = k.sb([128, 16, T], BF16, e1)
                hTt = [k.view(hT, (slice(None), slice(None), slice(t * 128, (t + 1) * 128))) for t in range(nt)]

                def hblk(b, kt):
                    return V(hT.ap[:, kt, b * BW:(b + 1) * BW], [d for t in range(b * tpb, (b + 1) * tpb) for d in hTt[t].deps])
                with ExitStack() as e2:
                    xt2 = [k.sb([128, D], F32, e2) for _ in range(2)]
                    xnb = k.sb([128, D], BF16, e2)
                    ssq = k.sb([128, 1], F32, e2); rs = k.sb([128, 1], F32, e2)
                    for t in range(nt):
                        xt = xt2[t % 2]
                        src = xp[t * 128:(t + 1) * 128, :] if mode == "P" else xs[:, :]
                        k.dma("sp", xt, src)
                        norm_to_T(xt, 128, gmix, hTt[t], 0, xnb, ssq, rs)
                k.fence()
                WG = k.sb([128, 16, 24], BF16, e1)
                k.dma("pool", WG[:, :, 0:16], w_in[:, C_DNB:C_DNB + 16].rearrange("(kt p) c -> p kt c", p=128))
                k.dma("pool", WG[:, :, 16:24], w_in[:, C_MLI:C_MLI + 8].rearrange("(kt p) c -> p kt c", p=128))
                GR = k.sb([128, nt, 24], F32, e1)
                GA_ = k.sb([128, nt, 32], F32, e1)
                for t in range(nt):
                    pg = brange(2, 256, 280)
                    for kt in range(16):
                        k.mm(pg, hTt[t][:, kt, :], WG[:, kt, :], start=(kt == 0), stop=(kt == 15))
                    k.cp("act", GR[:, t, :], pg)
                tmp = k.sb([128, nt, 8], F32, e1)
                b8 = lambda v: bc(V(v.ap.unsqueeze(1), v.deps), [128, nt, v.ap.shape[1]])
                k.act(GA_[:, :, 0:8], GR[:, :, 0:8], AF.Sigmoid)
                k.tt("dve", GA_[:, :, 24:28], GR[:, :, 16:20], b8(ib), ALU.add)
                k.tt("dve", tmp, GR[:, :, 8:16], b8(dtb), ALU.add)
                k.act(tmp, tmp, AF.Exp)
                k.act(tmp, tmp, AF.Ln, bias=onec)
                k.tt("dve", GA_[:, :, 16:24], tmp, b8(negA), ALU.mult)
                k.act(GA_[:, :, 8:16], GA_[:, :, 0:8], AF.Ln)
                k.tt("dve", tmp[:, :, 0:4], GR[:, :, 20:24], b8(fb), ALU.add)
                k.act(tmp[:, :, 0:4], tmp[:, :, 0:4], AF.Exp, scale=-1.0)
                k.act(tmp[:, :, 0:4], tmp[:, :, 0:4], AF.Ln, bias=onec)
                k.ts("dve", GA_[:, :, 28:32], tmp[:, :, 0:4], -1.0, None, ALU.mult)
                GAv = GA_

                def s128(n=1, dt=F32):
                    return k.sb([128, 128] if n == 1 else [128, n, 128], dt, e1)
                gs = k.sb([128, 8], F32, e1); ge = k.sb([128, 4], F32, e1)
                Dg = s128(2); t1 = s128(); t2 = s128(); t3 = s128(); E1 = s128(); E2 = s128(); E3 = s128()
                XX = [s128(2), s128(2)]; Pm = [s128(), s128()]; qkT = s128()
                kbg = s128(); kd = s128(); vb = s128(); wTn = s128(); vn = s128(); tA = s128(); osb = s128(); on = s128()
                junk1 = s128(); mv = k.sb([128, 128], F32, e1)
                sc = k.sb([128, 16], F32, e1)
                wbuf = [k.sb([128, 16, 128], BF16, e1) for _ in range(8)]
                wcols = []
                for h in range(8):
                    wcols += [C_DNQ + h * 128, C_DNK + h * 128, C_DNV + h * 128, C_DNZ + h * 128]
                for h in range(4):
                    wcols += [C_MLQ + h * 128, C_MLK + h * 128, C_MLV + h * 256, C_MLV + h * 256 + 128, C_MLO + h * 256, C_MLO + h * 256 + 128]
                wstate = {"loaded": 0}

                def ensure_w(base, n):
                    lim = min(len(wcols), base + 8)
                    while wstate["loaded"] < lim:
                        i_ = wstate["loaded"]
                        k.dma("pool", wbuf[i_ % 8], w_in[:, wcols[i_]:wcols[i_] + 128].rearrange("(kt p) c -> p kt c", p=128))
                        wstate["loaded"] += 1
                    return [wbuf[(base + j) % 8] for j in range(n)]
                if mode == "S":
                    STATE = k.sb([128, 16, 257], F32, e1)
                    Zw = s128(16); Zq = s128(16); KDx = s128(16)
                    k.memset("dve", Zw, 0.0); k.memset("dve", Zq, 0.0)
                    glbc = k.sb([128, 16], F32, e1); fsg = k.sb([128, 16], F32, e1)

                    def zdiag(z):
                        a = z.ap
                        return V(bass.AP(a.tensor, a.offset, [list(a.ap[0]), [136, 16], [1, 8]]), z.deps)

                    def blk8(v):
                        return V(v.ap.rearrange("p (r e) -> p r e", e=8), v.deps)
                BW = 512 if mode == "P" else 128
                nblk = T // BW
                nblk_run = min(nblk, DBG.get('nblk', 99))
                tpb = BW // 128

                def proj(w, b, dst_ps):
                    for kt in range(16):
                        k.mm(dst_ps, w[:, kt, :], hblk(b, kt), start=(kt == 0), stop=(kt == 15))

                def put_mix(e, t, val_ps, gcol, gate):
                    k.stt("dve", mv, val_ps, gcol, gate, ALU.mult, ALU.mult)
                    if mode == "S":
                        k.cp("act", mixT[:, e, 1026:1154], mv)
                    else:
                        c0 = 2 + (t % 8) * 128
                        dst = mixT[:, e, c0:c0 + 128]
                        if t < 8:
                            k.ts("dve", dst, mv, f0, None, ALU.mult)
                        else:
                            k.stt("dve", dst, mv, f1, dst, ALU.mult, ALU.add)
                        if t == 7:
                            k.ts("dve", mixT[:, e, 0:2], mv[:, 126:128], f1, None, ALU.mult)

                with ExitStack() as e3:
                    qT = k.sb([128, BW], F32, e3); kT = k.sb([128, BW], F32, e3); vT = k.sb([128, BW], F32, e3)
                    zs = k.sb([128, BW], F32, e3)
                    if mode == "P":
                        raws = [k.sb([128, 3 + BW], F32, e3) for _ in range(3)]
                    else:
                        raws = [k.sb([128, 16, 11], F32, e3) for _ in range(3)]
                        CSS = k.sb([128, 24, 16, 3], F32, e3)
                        k.memset("dve", CSS, 0.0)
                        dcs_in = k.sb([48, 3072], F32, e3)
                        k.dma("sp", dcs_in, sdc.rearrange("r t c -> (r t) c"))
                        Sout = k.sb([128, 16, 128], F32, e3)
                    cacc = k.sb([128, BW], F32, e3); sq = k.sb([128, BW], F32, e3); rst = k.sb([128, BW], F32, e3)
                    Sst = k.sb([128, 128], F32, e3)

                    for h in range(DBG.get('dn', 8)):
                        ws = ensure_w(4 * h, 4)
                        if mode == "P":
                            k.memset("dve", Sst, 0.0)
                        else:
                            k.dma("sp", STATE[:, :, 0:128], sS[:, h, :, :].rearrange("r k v -> k r v"))
                        for b in range(nblk_run):
                            for wi, dst in enumerate((qT, kT, vT)):
                                acc = brange(wi % 2, 0, BW)
                                proj(ws[wi], b, acc)
                                ct = wi * 8 + h
                                raw = raws[wi]
                                if mode == "P":
                                    if b == 0:
                                        k.memset("dve", raw[:, 0:3], 0.0)
                                    else:
                                        k.cp("dve", raw[:, 0:3], raw[:, BW:BW + 3])
                                    k.cp("act", raw[:, 3:3 + BW], acc)
                                    if b == nblk - 1:
                                        k.cp("dve", CSP[:, ct, :], raw[:, BW:BW + 3])
                                    sh = lambda j, raw=raw: raw[:, j:j + BW]
                                    ca = cacc
                                else:
                                    pt = brange(3, 0, 48)
                                    k.tr(pt, dcs_in[:, ct * 128:(ct + 1) * 128], ident[0:48, 0:48])
                                    k.cp("dve", raw[:, :, 0:3], V(pt.ap.rearrange("p (r t) -> p r t", t=3), pt.deps))
                                    k.cp("act", raw[:, :, 3:11], V(acc.ap.rearrange("p (r e) -> p r e", e=8), acc.deps))
                                    k.cp("dve", CSS[:, ct, :, :], raw[:, :, 8:11])
                                    sh = lambda j, raw=raw: raw[:, :, j:j + 8]
                                    ca = V(cacc.ap.rearrange("p (r e) -> p r e", e=8), cacc.deps)
                                k.ts("dve", ca, sh(0), dcw[:, ct, 0:1], None, ALU.mult)
                                for j in (1, 2, 3):
                                    k.stt("dve", ca, sh(j), dcw[:, ct, j:j + 1], ca, ALU.mult, ALU.add)
                                if wi == 2:
                                    k.act(vT, cacc, AF.Silu)
                                else:
                                    k.act(cacc, cacc, AF.Silu)
                                    k.tt("dve", sq, cacc, cacc, ALU.mult)
                                    pss = brange(4 + wi, 0, BW)
                                    k.mm(pss, ones, sq)
                                    k.act(rst, pss, AF.Ln, bias=epsc)
                                    k.act(rst, rst, AF.Exp, scale=-0.5)
                                    if wi == 0:
                                        k.stt("dve", dst, cacc, 128.0 ** -0.5, rst, ALU.mult, ALU.mult)
                                    else:
                                        k.tt("dve", dst, cacc, rst, ALU.mult)
                            acc = brange(1, 0, BW)
                            proj(ws[3], b, acc)
                            k.act(zs, acc, AF.Silu)

                            for tt_ in range(tpb):
                                t = b * tpb + tt_
                                sl = slice(tt_ * 128, (tt_ + 1) * 128)
                                q_, k_, v_, z_ = qT[:, sl], kT[:, sl], vT[:, sl], zs[:, sl]
                                if DBG.get('stage', 99) < 0:
                                    continue
                                g = GAv[:, t, 16 + h:17 + h]; beta = GAv[:, t, h:h + 1]; lnb = GAv[:, t, 8 + h:9 + h]
                                G2 = bq[2][2]
                                k.mm(G2[:, 0:1], M.U, g); k.mm(G2[:, 1:2], M.SL, g); k.mm(G2[:, 2:3], M.BO, g)
                                k.cp("act", gs[:, 0:3], G2[:, 0:3])
                                k.act(ge[:, 0:3], G2[:, 0:3], AF.Exp)
                                Gc, Glast = gs[:, 0:1], gs[:, 2:3]
                                eG, eGrev, glrow = ge[:, 0:1], ge[:, 1:2], ge[:, 2:3]
                                Gp, negGc, bg = gs[:, 3:4], gs[:, 4:5], gs[:, 5:6]
                                k.tt("dve", Gp, Gc, lnb, ALU.add)
                                k.ts("dve", negGc, Gc, -1.0, None, ALU.mult)
                                k.tt("dve", bg, beta, eG, ALU.mult)
                                if DBG.get('stage', 99) < 1:
                                    continue
                                k.ts("dve", Dg[:, 0, :], ident, Gc, None, ALU.mult)
                                k.ts("dve", Dg[:, 1, :], ident, Gp, None, ALU.mult)
                                Gb = brange(2, 0, 256)
                                k.mm(Gb, ones, V(Dg.ap.rearrange("p a b -> p (a b)"), Dg.deps))
                                k.tt("dve", t1, Gb[:, 0:128], M.NEGT_incl, ALU.add)
                                k.act(E1, t1, AF.Exp, bias=negGc)
                                k.tt("dve", t2, Gb[:, 128:256], M.NEGT_strict, ALU.add)
                                k.act(E2, t2, AF.Exp, bias=negGc)
                                k.stt("dve", t3, Gb[:, 0:128], -1.0, M.NEG_strict, ALU.mult, ALU.add)
                                k.act(E3, t3, AF.Exp, bias=Gp)
                                if DBG.get('stage', 99) < 2:
                                    continue
                                KKp, QKp = bq[3][0], bq[3][1]
                                k.mm(KKp, k_, k_); k.mm(QKp, k_, q_)
                                if DBG.get('stage', 99) < 3:
                                    continue
                                cur = 0
                                k.stt("dve", XX[0][:, 0, :], KKp, -1.0, E2, ALU.mult, ALU.mult)
                                k.stt("dve", XX[0][:, 1, :], KKp, -1.0, E3, ALU.mult, ALU.mult)
                                k.tt("dve", qkT, QKp, E1, ALU.mult)
                                pc = 0
                                k.tt("dve", Pm[0], XX[0][:, 0, :], ident, ALU.add)
                                for lv in range(nlev):
                                    X_, XT_ = XX[cur][:, 0, :], XX[cur][:, 1, :]
                                    k.mm(bq[4][0], XT_, X_); k.mm(bq[4][1], X_, XT_)
                                    nx = 1 - cur
                                    k.cp("act", V(XX[nx].ap.rearrange("p a b -> p (a b)"), XX[nx].deps), brange(4, 0, 256))
                                    k.mm(bq[4][2], XX[nx][:, 1, :], Pm[pc])
                                    k.tt("dve", Pm[1 - pc], Pm[pc], bq[4][2], ALU.add)
                                    cur = nx; pc = 1 - pc
                                if DBG.get('stage', 99) < 4:
                                    continue
                                PT = Pm[pc]
                                k.tr(bq[3][2], k_, ident); k.tr(bq[3][3], v_, ident)
                                if DBG.get('v', 0) == 1:
                                    k.ts("dve", kbg, bq[3][2], bg, None, ALU.mult)
                                    k.ts("dve", kd, bq[3][2], eGrev, None, ALU.mult)
                                    k.ts("dve", vb, bq[3][3], beta, None, ALU.mult)
                                elif DBG.get('v', 0) == 2:
                                    k.cp("dve", kbg, bq[3][2])
                                elif DBG.get('v', 0) == 3:
                                    pass
                                else:
                                    k.act(kbg, bq[3][2], AF.Copy, scale=bg)
                                    k.ts("dve", kd, bq[3][2], eGrev, None, ALU.mult)
                                    k.act(vb, bq[3][3], AF.Copy, scale=beta)
                                if DBG.get('stage', 99) < 5:
                                    continue
                                k.mm(bq[4][3], kbg, PT)
                                k.act(wTn, bq[4][3], AF.Copy, scale=-1.0)
                                if DBG.get('stage', 99) < 6:
                                    continue
                                vnp, Ap, Bp = bq[5][0], bq[5][1], bq[5][2]
                                if mode == "P":
                                    k.mm(vnp, PT, vb, start=True, stop=False)
                                    k.mm(vnp, wTn, Sst, start=False, stop=True)
                                    k.mm(Ap, q_, Sst)
                                    k.cp("dve", vn, vnp)
                                    k.mm(Bp, qkT, vn)
                                    k.mm(bq[6][0], kd, vn)
                                    k.stt("dve", Sst, Sst, glrow, bq[6][0], ALU.mult, ALU.add)
                                else:
                                    k.cp("act", zdiag(Zw), blk8(wTn))
                                    k.cp("act", zdiag(Zq), blk8(q_))
                                    k.tt("dve", KDx, bc(V(kd.ap.unsqueeze(1), kd.deps), [128, 16, 128]),
                                         bc(V(rowmask.ap.unsqueeze(2), rowmask.deps), [128, 16, 128]), ALU.mult)
                                    k.mm(vnp, PT, vb, start=True, stop=False)
                                    for r in range(16):
                                        k.mm(vnp, Zw[:, r, :], STATE[:, r, 0:128], start=False, stop=(r == 15))
                                    for r in range(16):
                                        k.mm(Ap, Zq[:, r, :], STATE[:, r, 0:128], start=(r == 0), stop=(r == 15))
                                    k.cp("dve", vn, vnp)
                                    k.mm(Bp, qkT, vn)
                                    k.ts("dve", fsg, firstsel, Glast, None, ALU.mult)
                                    k.mm(bq[2][3][:, 0:16], ones, fsg)
                                    k.act(glbc, bq[2][3][:, 0:16], AF.Exp)
                                    for r in range(16):
                                        dp = bq[6 + (r // 4) % 2][r % 4]
                                        k.mm(dp, KDx[:, r, :], vn)
                                        k.stt("dve", Sout[:, r, :], STATE[:, r, 0:128], glbc[:, r:r + 1], dp, ALU.mult, ALU.add)
                                    k.dma("sp", o_sS[:, h, :, :].rearrange("r k v -> k r v"), Sout)
                                if DBG.get('stage', 99) < 7:
                                    continue
                                k.act(tA, Ap, AF.Copy, scale=eG)
                                k.tt("dve", osb, tA, Bp, ALU.add)
                                k.act(junk1, osb, AF.Square, accum=sc[:, 0:1])
                                rstd_from_ss(sc[:, 1:2], sc[:, 0:1], 128)
                                k.act(on, osb, AF.Copy, scale=sc[:, 1:2])
                                k.tr(bq[5][3], on, ident)
                                put_mix(h, t, bq[5][3], gdn[:, 0:1], z_)
                        if mode == "P":
                            k.dma("sp", o_pS[h, :, :], Sst)
                    if DBG.get('noconvout'):
                        pass
                    elif mode == "P":
                        stg = k.sb([24, 3, 128], F32, e3)
                        for j in range(3):
                            pt = V(bq[3][0].ap[0:24, :], bq[3][0].deps)
                            k.op("pe", lambda: nc.tensor.matmul(pt.ap, lhsT=CSP.ap[:, :, j], rhs=ident.ap, start=True, stop=True), [pt], [CSP, ident])
                            k.cp("dve", stg[:, j, :], pt)
                        k.dma("sp", o_pdc.rearrange("j (t p) -> t j p", p=128), stg)
                    else:
                        stg = k.sb([48, 24, 128], F32, e3)
                        for ct in range(24):
                            pt = V(bq[3][ct % 2].ap[0:48, :], bq[3][ct % 2].deps)
                            k.tr(pt, V(CSS.ap[:, ct, :, :].rearrange("p r t -> p (r t)"), CSS.deps), ident)
                            k.cp("dve", stg[:, ct, :], pt)
                        k.dma("sp", o_sdc.rearrange("r t (c p) -> (r t) c p", p=128), stg)

                k.fence()
                with ExitStack() as e3:
                    qT = k.sb([128, BW], F32, e3); kT = k.sb([128, BW], F32, e3)
                    vT = k.sb([128, 2, BW], F32, e3); sg = k.sb([128, 2, BW], F32, e3)
                    Cst = k.sb([128, 257], F32, e3)
                    mrow = k.sb([128, 1], F32, e3)
                    vext = k.sb([128, 257], F32, e3)
                    k.memset("dve", vext[:, 256:257], 1.0)
                    Dm, Em, ssb, sT, kw = (k.sb([128, 128], F32, e3) for _ in range(5))
                    hsb = k.sb([128, 256], F32, e3); hn = k.sb([128, 256], F32, e3); tq = k.sb([128, 256], F32, e3)
                    ms = k.sb([128, 24], F32, e3)
                    if mode == "S":
                        m0 = k.sb([16, 4], F32, e3); k.dma("sp", m0, sm[:, :])
                        mout = k.sb([16, 4], F32, e3)
                        k.memset("dve", mout, 0.0)
                        KWx = KDx
                    for h in range(DBG.get('ml', 4)):
                        ws = ensure_w(32 + 6 * h, 6)
                        if mode == "P":
                            k.memset("dve", Cst, 0.0); k.memset("dve", mrow, 0.0)
                        else:
                            k.dma("sp", STATE[:, :, 0:256], sC[:, h, :, :].rearrange("r k v -> k r v"))
                            k.dma("sp", STATE[:, :, 256], sn[:, h, :].rearrange("r k -> k r"))
                        for b in range(nblk_run):
                            acc = brange(0, 0, BW); proj(ws[0], b, acc)
                            k.act(qT, acc, AF.Copy, scale=128.0 ** -0.5)
                            acc = brange(1, 0, BW); proj(ws[1], b, acc)
                            k.cp("dve", kT, acc)
                            for d2 in range(2):
                                acc = brange(0, 0, BW); proj(ws[2 + d2], b, acc)
                                k.cp("act", vT[:, d2, :], acc)
                                acc = brange(1, 0, BW); proj(ws[4 + d2], b, acc)
                                k.act(sg[:, d2, :], acc, AF.Sigmoid)
                            for tt_ in range(tpb):
                                t = b * tpb + tt_
                                sl = slice(tt_ * 128, (tt_ + 1) * 128)
                                q_, k_ = qT[:, sl], kT[:, sl]
                                ig = GAv[:, t, 24 + h:25 + h]; lf = GAv[:, t, 28 + h:29 + h]
                                G2 = bq[2][2]
                                k.mm(G2[:, 0:1], M.U, lf); k.mm(G2[:, 1:2], M.SL, lf); k.mm(G2[:, 2:3], M.BO, lf)
                                k.cp("act", ms[:, 0:3], G2[:, 0:3])
                                bcs, brev, btot = ms[:, 0:1], ms[:, 1:2], ms[:, 2:3]
                                if mode == "S":
                                    k.mm(G2[:, 4:5], seqsel, m0[:, h:h + 1])
                                    k.cp("act", mrow, G2[:, 4:5])
                                a_ = ms[:, 3:4]; bm = ms[:, 4:5]; rmax = ms[:, 5:6]; mnew = ms[:, 6:7]; negm = ms[:, 7:8]
                                inter = ms[:, 8:9]; mend = ms[:, 9:10]; wend = ms[:, 10:11]; carry = ms[:, 11:12]
                                den = ms[:, 12:13]; emn = ms[:, 13:14]; rd = ms[:, 14:15]; ir = ms[:, 15:16]; ssq_ = ms[:, 16:17]; rs_ = ms[:, 17:18]
                                tmpc = ms[:, 18:19]
                                k.tt("dve", a_, ig, bcs, ALU.subtract)
                                k.ts("dve", Dg[:, 0, :], ident, a_, None, ALU.mult)
                                k.mm(bq[2][0], ones, Dg[:, 0, :])
                                k.stt("dve", Dm, bq[2][0], bcs, M.NEG_incl, ALU.add, ALU.add)
                                k.redmax(rmax, Dm)
                                k.tt("dve", bm, bcs, mrow, ALU.add)
                                k.tt("dve", mnew, bm, rmax, ALU.max)
                                k.ts("dve", negm, mnew, -1.0, None, ALU.mult)
                                k.tt("dve", tmpc, bm, mnew, ALU.subtract)
                                k.act(inter, tmpc, AF.Exp)
                                k.act(Em, Dm, AF.Exp, bias=negm)
                                k.mm(bq[3][0], q_, k_)
                                k.tt("dve", ssb, bq[3][0], Em, ALU.mult)
                                k.tr(bq[3][1], ssb, ident)
                                k.cp("act", sT, bq[3][1])
                                k.tr(bq[3][2], k_, ident)
                                k.tr(bq[4][0], vT[:, 0, sl], ident); k.tr(bq[4][1], vT[:, 1, sl], ident)
                                k.cp("act", vext[:, 0:256], brange(4, 0, 256))
                                k.mm(G2[:, 8:9], M.LAST, mnew)
                                k.cp("act", mend, G2[:, 8:9])
                                k.tt("dve", tmpc, brev, ig, ALU.add)
                                k.tt("dve", tmpc, tmpc, mend, ALU.subtract)
                                k.act(wend, tmpc, AF.Exp)
                                k.tt("dve", carry, btot, mrow, ALU.add)
                                k.tt("dve", carry, carry, mend, ALU.subtract)
                                k.ts("dve", kw, bq[3][2], wend, None, ALU.mult)
                                QC = brange(7, 0, 257); SV = brange(6, 0, 257)
                                if mode == "P":
                                    k.mm(QC, q_, Cst)
                                else:
                                    k.cp("act", zdiag(Zq), blk8(q_))
                                    for r in range(16):
                                        k.mm(QC, Zq[:, r, :], STATE[:, r, :], start=(r == 0), stop=(r == 15))
                                k.mm(SV, sT, vext)
                                k.tt("dve", den, QC[:, 256:257], inter, ALU.mult)
                                k.tt("dve", den, den, SV[:, 256:257], ALU.add)
                                k.act(den, den, AF.Abs)
                                k.act(emn, negm, AF.Exp)
                                k.tt("dve", den, den, emn, ALU.max)
                                k.recip(rd, den)
                                k.tt("dve", ir, inter, rd, ALU.mult)
                                k.act(tq, QC[:, 0:256], AF.Copy, scale=ir)
                                k.stt("dve", hsb, SV[:, 0:256], rd, tq, ALU.mult, ALU.add)
                                k.act(tq, hsb, AF.Square, accum=ssq_)
                                rstd_from_ss(rs_, ssq_, 256)
                                k.act(hn, hsb, AF.Copy, scale=rs_)
                                for d2 in range(2):
                                    k.tr(bq[5][d2], hn[:, d2 * 128:(d2 + 1) * 128], ident)
                                    put_mix(8 + 2 * h + d2, t, bq[5][d2], gml[:, 2 * h + d2:2 * h + d2 + 1], sg[:, d2, sl])
                                if mode == "P":
                                    k.act(carry, carry, AF.Exp)
                                    k.mm(SV, kw, vext)
                                    k.stt("dve", Cst, Cst, carry, SV, ALU.mult, ALU.add)
                                    k.cp("dve", mrow, mend)
                                else:
                                    k.tt("dve", KWx, bc(V(kw.ap.unsqueeze(1), kw.deps), [128, 16, 128]),
                                         bc(V(rowmask.ap.unsqueeze(2), rowmask.deps), [128, 16, 128]), ALU.mult)
                                    k.ts("dve", fsg, firstsel, carry, None, ALU.mult)
                                    k.mm(bq[2][3][:, 0:16], ones, fsg)
                                    k.act(glbc, bq[2][3][:, 0:16], AF.Exp)
                                    for r in range(16):
                                        dp = brange(6 + r % 2, 0, 257)
                                        k.mm(dp, KWx[:, r, :], vext)
                                        k.stt("dve", STATE[:, r, :], STATE[:, r, :], glbc[:, r:r + 1], dp, ALU.mult, ALU.add)
                                    k.dma("sp", o_sC[:, h, :, :].rearrange("r k v -> k r v"), STATE[:, :, 0:256])
                                    k.dma("sp", o_sn[:, h, :].rearrange("r k -> k r"), STATE[:, :, 256])
                                    pm_ = V(G2.ap[0:16, 12:13], G2.deps)
                                    k.mm(pm_, firstsel, mend)
                                    k.cp("dve", mout[:, h:h + 1], pm_)
                        if mode == "P":
                            k.dma("sp", o_pC[h, :, :], Cst[:, 0:256])
                            k.dma("sp", o_pn[h, :].rearrange("(k o) -> k o", o=1), Cst[:, 256:257])
                            k.dma("sp", o_pm[0:1, h:h + 1], mrow[0:1, 0:1])
                    if mode == "S":
                        k.dma("sp", o_sm[:, :], mout)
                k.fence()


        for md in DBG.get("modes", ["P", "S"]):
            run_mixer(md)
            k.fence()
        eA.close()
        k.fence()

        with ExitStack() as eb:
            X1 = [k.sb([128, D], F32, eb) for _ in range(3)]
            xtmp = k.sb([128, D], F32, eb)
            h2T = k.sb([128, 16, 386], BF16, eb)
            actT = k.sb([128, 44, 386], BF16, eb)
            Wo = [k.sb([128, 16, 256], BF16, eb) for _ in range(2)]
            Wu = [k.sb([128, 16, 128], BF16, eb) for _ in range(4)]
            Wd = [k.sb([128, 44, 128], BF16, eb) for _ in range(2)]
            gfin = k.sb([128, D], F32, eb)
            k.dma("sp", gfin, norm_final_g.partition_broadcast(128))
            fcw = k.sb([128, 88, 3], F32, eb); fcb = k.sb([128, 88], F32, eb)
            for j in range(3):
                k.dma("sp", fcw[:, :, j], ffn_conv_w[j, :].rearrange("(t p) -> p t", p=128))
            k.dma("sp", fcb, ffn_conv_b.rearrange("(t p) -> p t", p=128))
            CF = k.sb([128, 88, 2], F32, eb)
            Rb = [k.sb([128, 388], F32, eb) for _ in range(2)]
            Cb = [k.sb([128, 386], F32, eb) for _ in range(2)]
            RS = [k.sb([128, 16, 10], F32, eb) for _ in range(2)]
            so32 = k.sb([128, 32], F32, eb)
            stgs = [k.sb([32, 128], F32, eb) for _ in range(2)]
            PFO = k.sb([128, 88, 2], F32, eb)
            xnb = k.sb([128, D], BF16, eb)
            ssq = k.sb([128, 1], F32, eb); rs = k.sb([128, 1], F32, eb)
            yT = k.sb([128, 384], F32, eb)
            sfi = [k.sb([32, 128], F32, eb) for _ in range(2)]
            stg = k.sb([88, 2, 128], F32, eb)
            wctr = {"o": 0, "u": 0, "d": 0}

            blocks = [
                (0, 386, 2, [(2, xh[0:2, :], 0, None)] + [(128, xo[j * 128:(j + 1) * 128, :], 2 + j * 128, j * 128) for j in range(3)]),
                (386, 384, 0, [(128, xo[j * 128:(j + 1) * 128, :], (j - 3) * 128, j * 128) for j in (3, 4, 5)]),
                (770, 384, 0, [(128, xo[j * 128:(j + 1) * 128, :], (j - 6) * 128, j * 128) for j in (6, 7, 8)]),
            ]
            for bi, (c0, Wb, cs, tiles) in enumerate(blocks[:DBG.get('B', 3)]):
                tbuf = []
                xi = 0
                for (np_, src, col, orow) in tiles:
                    if orow is None:
                        xt = xtmp
                    else:
                        xt = X1[xi]; xi += 1
                    k.dma("sp", xt[0:np_, :], src)
                    tbuf.append(xt)
                for db in range(8):
                    wo = Wo[wctr["o"] % 2]; wctr["o"] += 1
                    k.dma("pool", wo, w_out[:, db * 256:(db + 1) * 256].rearrange("(e p) c -> p e c", p=128))
                    for ti, (np_, src, col, orow) in enumerate(tiles):
                        ps = V(banks[2 + ti % 2][0:np_, 0:256], [bdep[2 + ti % 2]])
                        for e in range(16):
                            k.mm(ps, mixT[:, e, c0 + col:c0 + col + np_], wo[:, e, :], start=(e == 0), stop=(e == 15))
                        xs_ = tbuf[ti][0:np_, db * 256:(db + 1) * 256]
                        k.tt("dve", xs_, xs_, ps, ALU.add)
                for ti, (np_, src, col, orow) in enumerate(tiles):
                    norm_to_T(tbuf[ti], np_, gffn, h2T, col, xnb, ssq, rs)
                for ft in range(44):
                    accs = []
                    for which in range(2):
                        wu = Wu[wctr["u"] % 4]; wctr["u"] += 1
                        cb0 = which * DFF + ft * 128
                        k.dma("pool", wu, w_up[:, cb0:cb0 + 128].rearrange("(kt p) c -> p kt c", p=128))
                        acc = brange(which, 0, Wb)
                        for kt in range(16):
                            k.mm(acc, wu[:, kt, :], h2T[:, kt, 0:Wb], start=(kt == 0), stop=(kt == 15))
                        accs.append(acc)
                    for which in range(2):
                        ct = which * 44 + ft
                        acc = accs[which]; R = Rb[which]; C = Cb[which]
                        w0, w1, w2, bb = fcw[:, ct, 0:1], fcw[:, ct, 1:2], fcw[:, ct, 2:3], fcb[:, ct:ct + 1]
                        if bi == 0:
                            k.cp("act", R[:, 0:386], acc)
                            L = 384
                            k.cp("act", CF[:, ct, :], R[:, 384:386])
                        elif bi == 1:
                            k.cp("dve", R[:, 0:2], CF[:, ct, :])
                            k.cp("act", R[:, 2:386], acc)
                            L = 384
                            k.cp("act", CF[:, ct, :], R[:, 384:386])
                        else:
                            k.cp("dve", R[:, 0:2], CF[:, ct, :])
                            k.cp("act", R[:, 2:258], acc[:, 0:256])
                            L = 256
                            k.cp("act", PFO[:, ct, :], R[:, 256:258])
                            sf = sfi[ct % 2]
                            k.dma("sp", sf, sfc.rearrange("r t c -> (r t) c")[:, ct * 128:(ct + 1) * 128])
                            pt = V(banks[3][:, 0:32], bq[3][0].deps)
                            k.tr(pt, sf, ident[0:32, 0:32])
                            rs_ = RS[which]
                            k.cp("dve", rs_[:, :, 0:2], V(pt.ap.rearrange("p (r t) -> p r t", t=2), pt.deps))
                            k.cp("act", rs_[:, :, 2:10], V(acc.ap[:, 256:384].rearrange("p (r e) -> p r e", e=8), acc.deps))
                            k.cp("act", V(so32.ap.rearrange("p (r t) -> p r t", t=2), so32.deps), rs_[:, :, 8:10])
                            pt2 = V(banks[3][0:32, 128:256], bq[3][1].deps)
                            k.tr(pt2, so32, ident)
                            k.cp("dve", stgs[ct % 2], pt2)
                            k.dma("sp", o_sfc.rearrange("r t c -> (r t) c")[:, ct * 128:(ct + 1) * 128], stgs[ct % 2])
                            c3 = V(C.ap[:, 256:384].rearrange("p (r e) -> p r e", e=8), C.deps)
                            k.ts("dve", c3, rs_[:, :, 0:8], w0, bb, ALU.mult, ALU.add)
                            k.stt("dve", c3, rs_[:, :, 1:9], w1, c3, ALU.mult, ALU.add)
                            k.stt("dve", c3, rs_[:, :, 2:10], w2, c3, ALU.mult, ALU.add)
                        k.ts("dve", C[:, 0:L], R[:, 0:L], w0, bb, ALU.mult, ALU.add)
                        k.stt("dve", C[:, 0:L], R[:, 1:L + 1], w1, C[:, 0:L], ALU.mult, ALU.add)
                        k.stt("dve", C[:, 0:L], R[:, 2:L + 2], w2, C[:, 0:L], ALU.mult, ALU.add)
                    k.act(Cb[0][:, 0:384], Cb[0][:, 0:384], AF.Silu)
                    k.tt("dve", actT[:, ft, cs:cs + 384], Cb[0][:, 0:384], Cb[1][:, 0:384], ALU.mult)
                otiles = [(tb, col, orow) for tb, (np_, src, col, orow) in zip(tbuf, tiles) if orow is not None]
                for dt_ in range(16):
                    wd = Wd[wctr["d"] % 2]; wctr["d"] += 1
                    k.dma("pool", wd, w_down[:, dt_ * 128:(dt_ + 1) * 128].rearrange("(f p) c -> p f c", p=128))
                    acc = brange(dt_ % 2, 0, 384)
                    for ft in range(44):
                        k.mm(acc, wd[:, ft, :], actT[:, ft, cs:cs + 384], start=(ft == 0), stop=(ft == 43))
                    k.cp("act", yT, acc)
                    for j, (tb, col, orow) in enumerate(otiles):
                        pt = bq[4 + j % 2][j // 2]
                        k.tr(pt, yT[:, j * 128:(j + 1) * 128], ident)
                        xs_ = tb[:, dt_ * 128:(dt_ + 1) * 128]
                        k.tt("dve", xs_, xs_, pt, ALU.add)
                for (tb, col, orow) in otiles:
                    k.act(xnb, tb, AF.Square, accum=ssq)
                    rstd_from_ss(rs, ssq, D)
                    k.stt("dve", tb, tb, rs, gfin, ALU.mult, ALU.mult)
                    k.dma("sp", yo[orow:orow + 128, :], tb)
            for t2 in range(2 if DBG.get('B', 3) == 3 else 0):
                pt = V(banks[3][0:88, 0:128], bq[3][0].deps)
                k.op("pe", lambda: nc.tensor.matmul(pt.ap, lhsT=PFO.ap[:, :, t2], rhs=ident.ap, start=True, stop=True), [pt], [PFO, ident])
                k.cp("dve", stg[:, t2, :], pt)
            if DBG.get('B', 3) == 3:
                k.dma("sp", o_pfc.rearrange("t (c p) -> c t p", p=128), stg[:, 0:2, :])
        k.finish()
    return nc


_CACHE = {}


def _consts():
    c = np.zeros((128, 20, 128), np.float32)
    idx = np.arange(128)
    c[:, 0, :] = np.eye(128)
    c[:, 1, :] = 1.0
    for o, bs in ((2, 128), (10, 8)):
        blk = idx // bs
        same = blk[:, None] == blk[None, :]
        kk = idx[:, None]; ii = idx[None, :]
        c[:, o + 0, :] = same & (kk <= ii)
        c[:, o + 1, :] = same & (kk > ii)
        c[:, o + 2, :] = same
        c[:, o + 3, :] = (kk == (blk[None, :] * bs + bs - 1))
        c[:, o + 4, :] = np.where(same & (kk <= ii), 0.0, NEG)
        c[:, o + 5, :] = np.where(same & (kk < ii), 0.0, NEG)
        c[:, o + 6, :] = np.where(same & (ii < kk), 0.0, NEG)
        c[:, o + 7, :] = np.where(same & (ii <= kk), 0.0, NEG)
    seq = idx // 8
    c[:, 18, 0:16] = seq[:, None] == np.arange(16)[None, :]
    c[:, 18, 16:32] = (idx[:, None] == (np.arange(16)[None, :] * 8))
    c[0:16, 19, :] = np.arange(16)[:, None] == seq[None, :]
    return c


def kernel(x_prompt, x_sample, state_dn_conv, state_dn_S, state_ml_C, state_ml_n, state_ml_m,
           state_ffn_conv, norm_mix_g, w_in, dn_conv_w, dn_A_log, dn_dt_bias, dn_norm_g,
           ml_i_bias, ml_f_bias, ml_norm_g, w_out, norm_ffn_g, w_up, ffn_conv_w, ffn_conv_b,
           w_down, norm_final_g):
    if "nc" not in _CACHE:
        _CACHE["nc"] = build_program()
    nc = _CACHE["nc"]
    in_maps = make_in_maps(**{k_: v_ for k_, v_ in locals().items() if k_ != "nc"})
    res = run_bass_kernel_spmd(nc, in_maps, core_ids=list(range(8))).results
    return assemble(res)


def make_in_maps(x_prompt, x_sample, state_dn_conv, state_dn_S, state_ml_C, state_ml_n, state_ml_m,
                 state_ffn_conv, norm_mix_g, w_in, dn_conv_w, dn_A_log, dn_dt_bias, dn_norm_g,
                 ml_i_bias, ml_f_bias, ml_norm_g, w_out, norm_ffn_g, w_up, ffn_conv_w, ffn_conv_b,
                 w_down, norm_final_g):
    f = lambda a: np.ascontiguousarray(np.asarray(a, dtype=np.float32))
    x_prompt = f(x_prompt); x_sample = f(x_sample)
    cst = _consts()
    shared = dict(
        cst=cst, norm_mix_g=f(norm_mix_g)[0], w_in=f(w_in)[0], dn_conv_w=f(dn_conv_w)[0], dn_A_log=f(dn_A_log)[0],
        dn_dt_bias=f(dn_dt_bias)[0], dn_norm_g=f(dn_norm_g)[0], ml_i_bias=f(ml_i_bias)[0], ml_f_bias=f(ml_f_bias)[0],
        ml_norm_g=f(ml_norm_g)[0], w_out=f(w_out)[0], norm_ffn_g=f(norm_ffn_g)[0], w_up=f(w_up)[0],
        ffn_conv_w=f(ffn_conv_w)[0], ffn_conv_b=f(ffn_conv_b)[0], w_down=f(w_down)[0], norm_final_g=f(norm_final_g))
    sdc = f(state_dn_conv)[0]; sS = f(state_dn_S)[0]; sC = f(state_ml_C)[0]; sn = f(state_ml_n)[0]
    sm = f(state_ml_m)[0]; sfc = f(state_ffn_conv)[0]
    in_maps = []
    for c in range(8):
        s, hf = c // 2, c % 2
        r0, r1 = 16 * c, 16 * c + 16
        xsamp = x_sample[r0:r1].reshape(128, D)
        xo = np.concatenate([x_prompt[s, hf * 1024:(hf + 1) * 1024], xsamp], axis=0)
        xh = x_prompt[s, 1022:1024].copy() if hf == 1 else np.zeros((2, D), np.float32)
        selv = np.zeros((128, 2), np.float32); selv[:, hf] = 1.0
        m = dict(shared)
        m.update(xp=x_prompt[s], xs=xsamp, xo=np.ascontiguousarray(xo), xh=xh, sel=selv,
                 sdc=sdc[r0:r1], sS=sS[r0:r1], sC=sC[r0:r1], sn=sn[r0:r1], sm=sm[r0:r1], sfc=sfc[r0:r1])
        in_maps.append({kk: np.ascontiguousarray(v) for kk, v in m.items()})
    return in_maps


def assemble(res):
    y_prompt = np.zeros((4, 2048, D), np.float32); y_sample = np.zeros((128, 8, D), np.float32)
    for c in range(8):
        s, hf = c // 2, c % 2
        y_prompt[s, hf * 1024:(hf + 1) * 1024] = res[c]["yo"][0:1024]
        y_sample[16 * c:16 * c + 16] = res[c]["yo"][1024:1152].reshape(16, 8, D)
    odd = [res[2 * s + 1] for s in range(4)]
    st = lambda name: np.stack([o[name] for o in odd], 0)[None]
    p_dn_conv = st("o_pdc"); p_dn_S = st("o_pS"); p_ml_C = st("o_pC"); p_ml_n = st("o_pn")
    p_ml_m = np.stack([o["o_pm"][0] for o in odd], 0)[None]
    p_ffn = st("o_pfc")
    cat = lambda name: np.concatenate([res[c][name] for c in range(8)], 0)[None]
    return (y_prompt, y_sample, p_dn_conv, p_dn_S, p_ml_C, p_ml_n, p_ml_m, p_ffn,
            cat("o_sdc"), cat("o_sS"), cat("o_sC"), cat("o_sn"), cat("o_sm"), cat("o_sfc"))
```

```python
import numpy as np
from contextlib import ExitStack
import concourse.bass as bass
import concourse.mybir as mybir
from concourse.bass_utils import run_bass_kernel_spmd

F32 = mybir.dt.float32
BF16 = mybir.dt.bfloat16
AF = mybir.ActivationFunctionType
ALU = mybir.AluOpType
AX = mybir.AxisListType

D = 2048
NPT = 16
EPS = 1e-6
NEG = -30000.0
SAME_ENG_SYNC = True
C_DNQ, C_DNK, C_DNV, C_DNZ, C_DNB, C_DNA = 0, 1024, 2048, 3072, 4096, 4104
C_MLQ, C_MLK, C_MLV, C_MLO, C_MLI, C_MLF = 4112, 4624, 5136, 6160, 7184, 7188
DFF = 5632


class Dep:
    __slots__ = ("w", "r", "sem", "cnt", "key", "excl")

    def __init__(self):
        self.excl = False
        self.w = None
        self.r = {}
        self.sem = None
        self.cnt = 0
        self.key = None


class V:
    def __init__(self, ap, deps):
        self.ap = ap
        self.deps = deps

    def __getitem__(self, idx):
        return V(self.ap[idx], self.deps)


class K:
    def __init__(self, nc, es):
        self.nc = nc
        self.es = es
        self.eng = {"pe": nc.tensor, "dve": nc.vector, "act": nc.scalar, "pool": nc.gpsimd, "sp": nc.sync}
        self.sem = {e: es.enter_context(nc.semaphore("s_" + e)) for e in ("pe", "dve", "act", "pool")}
        self.cnt = {e: 0 for e in self.sem}
        self.waited = {e: {} for e in self.eng}
        self.pending = {e: [] for e in self.sem}
        self.nsem = 0
        self.dma_deps = []
        self.nalloc = 0
        self.barrier = {}

    def newdep(self):
        d = Dep()
        d.r = dict(self.barrier)
        return d

    def fence(self):
        for e in self.sem:
            if self.cnt[e] > 0:
                self.barrier[e] = (e, self.sem[e], self.cnt[e])
        for sd in self.dma_deps:
            self.barrier[sd.key] = (sd.key, sd.sem, sd.cnt)

    def sb(self, shape, dt=F32, es=None):
        self.nalloc += 1
        t = (es or self.es).enter_context(self.nc.sbuf_tensor("t%d" % self.nalloc, list(shape), dt))
        return V(t[:], [self.newdep()])

    def psb(self, shape, dt=F32, es=None):
        self.nalloc += 1
        t = (es or self.es).enter_context(self.nc.psum_tensor("p%d" % self.nalloc, list(shape), dt))
        return t

    def view(self, v, idx):
        return V(v.ap[idx], [self.newdep()])

    def _wait(self, E, ev):
        if ev is None:
            return
        key, sem, val = ev
        if key == E and (E == "pe" or not SAME_ENG_SYNC):
            return
        if self.waited[E].get(key, 0) >= val:
            return
        self.eng[E].wait_ge(sem, val)
        self.waited[E][key] = val

    def _pre(self, E, R, W):
        for d in R:
            self._wait(E, d.w)
            if d.excl:
                for ek, ev in list(d.r.items()):
                    if ek != E:
                        self._wait(E, ev)
        for d in W:
            self._wait(E, d.w)
            for ev in list(d.r.values()):
                self._wait(E, ev)

    def op(self, E, fn, outs, ins, inc=True):
        R = [d for v in ins if isinstance(v, V) for d in v.deps]
        W = [d for v in outs if isinstance(v, V) for d in v.deps]
        self._pre(E, R, W)
        ins_ = fn()
        if not inc:
            self.pending[E].append((R, W))
            return
        self.cnt[E] += 1
        ins_.then_inc(self.sem[E], 1)
        ev = (E, self.sem[E], self.cnt[E])
        for (R2, W2) in self.pending[E] + [(R, W)]:
            for d in R2:
                d.r[E] = ev
            for d in W2:
                d.w = ev
                d.r = {}
        self.pending[E] = []

    def dma(self, Q, out, in_, **kw):
        R = in_.deps if isinstance(in_, V) else []
        W = out.deps if isinstance(out, V) else []
        self._pre(Q, R, W)
        sd = (W or R)[0]
        if sd.sem is None:
            self.nsem += 1
            sd.sem = self.es.enter_context(self.nc.semaphore("d%d" % self.nsem))
            sd.key = "d%d" % self.nsem
            self.dma_deps.append(sd)
        o = out.ap if isinstance(out, V) else out
        i = in_.ap if isinstance(in_, V) else in_
        self.eng[Q].dma_start(out=o, in_=i, **kw).then_inc(sd.sem, 16)
        sd.cnt += 16
        ev = (sd.key, sd.sem, sd.cnt)
        for d in R:
            d.r[sd.key] = ev
        for d in W:
            d.w = ev
            d.r = {}

    def finish(self):
        for sd in self.dma_deps:
            self.nc.sync.wait_ge(sd.sem, sd.cnt)

    @staticmethod
    def _a(x):
        return x.ap if isinstance(x, V) else x

    def mm(self, out, lhsT, rhs, start=True, stop=True):
        self.op("pe", lambda: self.nc.tensor.matmul(out.ap, lhsT=lhsT.ap, rhs=rhs.ap, start=start, stop=stop),
                [out], [lhsT, rhs], inc=stop)

    def tr(self, out, in_, ident):
        if in_.ap.dtype == F32:
            self.op("pe", lambda: self.nc.tensor.matmul(out.ap, lhsT=in_.ap, rhs=ident.ap, start=True, stop=True), [out], [in_, ident])
        else:
            self.op("pe", lambda: self.nc.tensor.transpose(out=out.ap, in_=in_.ap, identity=ident.ap), [out], [in_, ident])

    def act(self, out, in_, func, bias=None, scale=None, accum=None):
        kw = {}
        if bias is not None:
            kw["bias"] = self._a(bias)
        if scale is not None:
            kw["scale"] = self._a(scale)
        if accum is not None:
            kw["accum_out"] = accum.ap
        outs = [out] + ([accum] if accum is not None else [])
        self.op("act", lambda: self.nc.scalar.activation(out=out.ap, in_=in_.ap, func=func, **kw), outs, [in_, bias, scale])

    def ts(self, E, out, in0, s1, s2, op0, op1=None):
        e = self.eng[E]
        if op1 is None:
            self.op(E, lambda: e.tensor_scalar(out=out.ap, in0=in0.ap, scalar1=self._a(s1), scalar2=None, op0=op0), [out], [in0, s1])
        else:
            self.op(E, lambda: e.tensor_scalar(out=out.ap, in0=in0.ap, scalar1=self._a(s1), scalar2=self._a(s2), op0=op0, op1=op1),
                    [out], [in0, s1, s2])

    def stt(self, E, out, in0, scalar, in1, op0, op1):
        e = self.eng[E]
        self.op(E, lambda: e.scalar_tensor_tensor(out=out.ap, in0=in0.ap, scalar=self._a(scalar), in1=in1.ap, op0=op0, op1=op1),
                [out], [in0, scalar, in1])

    def tt(self, E, out, in0, in1, op):
        e = self.eng[E]
        self.op(E, lambda: e.tensor_tensor(out=out.ap, in0=in0.ap, in1=in1.ap, op=op), [out], [in0, in1])

    def cp(self, E, out, in_):
        if E == "act":
            self.act(out, in_, AF.Copy)
        else:
            e = self.eng[E]
            self.op(E, lambda: e.tensor_copy(out=out.ap, in_=in_.ap), [out], [in_])

    def redmax(self, out, in_):
        self.op("dve", lambda: self.nc.vector.reduce_max(out=out.ap, in_=in_.ap, axis=AX.X), [out], [in_])

    def recip(self, out, in_):
        self.op("dve", lambda: self.nc.vector.reciprocal(out=out.ap, in_=in_.ap), [out], [in_])

    def memset(self, E, out, val):
        e = self.eng[E]
        self.op(E, lambda: e.memset(out.ap, val), [out], [])


def bc(v, shape):
    return V(v.ap.to_broadcast(list(shape)), v.deps)


DBG = {}


def build_program():
    nc = bass.Bass("TRN2", target_bir_lowering=False)

    def din(name, shape, dt=F32):
        return nc.dram_tensor(name, list(shape), dt, kind="ExternalInput").ap()

    def dout(name, shape):
        return nc.dram_tensor(name, list(shape), F32, kind="ExternalOutput").ap()

    xp = din("xp", [2048, D]); xs = din("xs", [128, D]); xo = din("xo", [1152, D]); xh = din("xh", [2, D])
    sel = din("sel", [128, 2]); cst = din("cst", [128, 20, 128])
    sdc = din("sdc", [16, 3, 3072]); sS = din("sS", [16, 8, 128, 128]); sC = din("sC", [16, 4, 128, 256])
    sn = din("sn", [16, 4, 128]); sm = din("sm", [16, 4]); sfc = din("sfc", [16, 2, 2 * DFF])
    norm_mix_g = din("norm_mix_g", [D]); w_in = din("w_in", [D, 7192]); dn_conv_w = din("dn_conv_w", [4, 3072])
    dn_A_log = din("dn_A_log", [8]); dn_dt_bias = din("dn_dt_bias", [8]); dn_norm_g = din("dn_norm_g", [128])
    ml_i_bias = din("ml_i_bias", [4]); ml_f_bias = din("ml_f_bias", [4]); ml_norm_g = din("ml_norm_g", [1024])
    w_out = din("w_out", [D, D]); norm_ffn_g = din("norm_ffn_g", [D]); w_up = din("w_up", [D, 2 * DFF])
    ffn_conv_w = din("ffn_conv_w", [3, 2 * DFF]); ffn_conv_b = din("ffn_conv_b", [2 * DFF]); w_down = din("w_down", [DFF, D])
    norm_final_g = din("norm_final_g", [D])

    yo = dout("yo", [1152, D])
    o_pdc = dout("o_pdc", [3, 3072]); o_pS = dout("o_pS", [8, 128, 128]); o_pC = dout("o_pC", [4, 128, 256])
    o_pn = dout("o_pn", [4, 128]); o_pm = dout("o_pm", [1, 4]); o_pfc = dout("o_pfc", [2, 2 * DFF])
    o_sdc = dout("o_sdc", [16, 3, 3072]); o_sS = dout("o_sS", [16, 8, 128, 128]); o_sC = dout("o_sC", [16, 4, 128, 256])
    o_sn = dout("o_sn", [16, 4, 128]); o_sm = dout("o_sm", [16, 4]); o_sfc = dout("o_sfc", [16, 2, 2 * DFF])

    with ExitStack() as es:
        es.enter_context(nc.allow_non_contiguous_dma(reason="small strided param/state transfers"))
        k = K(nc, es)
        banks = [k.psb([128, 512]) for _ in range(8)]
        bdep = [Dep() for _ in range(8)]
        for d_ in bdep:
            d_.excl = True
        bq = [[V(banks[b][:, q * 128:(q + 1) * 128], [bdep[b]]) for q in range(4)] for b in range(8)]

        def brange(b, c0, c1):
            return V(banks[b][:, c0:c1], [bdep[b]])

        def ptb(b):
            return V(banks[b][:, :].bitcast(BF16).rearrange("p (a c) -> p a c", c=128), [bdep[b]])
        PTB = [ptb(0), ptb(1)]
        ident = k.sb([128, 128])
        k.dma("sp", ident, cst[:, 0, :])
        identb = k.sb([128, 128], BF16)
        k.cp("dve", identb, ident)
        SELT = k.sb([128, 2]); k.dma("sp", SELT, sel[:, :])
        f0 = SELT[:, 0:1]; f1 = SELT[:, 1:2]
        cc = k.sb([128, 4])
        k.memset("dve", cc[:, 0:1], EPS); k.memset("dve", cc[:, 1:2], 1.0); k.memset("dve", cc[:, 2:3], 0.0)
        epsc = cc[:, 0:1]; onec = cc[:, 1:2]

        def pload_bc(src, n):
            t = k.sb([128, n])
            k.dma("sp", t, src.partition_broadcast(128))
            return t

        def pload_fm(src, nt):
            t = k.sb([128, nt])
            k.dma("sp", t, src.rearrange("(t p) -> p t", p=128))
            return t
        gmix = pload_fm(norm_mix_g, 16); gffn = pload_fm(norm_ffn_g, 16)
        gdn = pload_fm(dn_norm_g, 1); gml = pload_fm(ml_norm_g, 8)
        Alog = pload_bc(dn_A_log, 8); dtb = pload_bc(dn_dt_bias, 8); ib = pload_bc(ml_i_bias, 4); fb = pload_bc(ml_f_bias, 4)
        negA = k.sb([128, 8])
        k.act(negA, Alog, AF.Exp)
        k.ts("dve", negA, negA, -1.0, None, ALU.mult)
        dcw = k.sb([128, 24, 4])
        for j in range(4):
            k.dma("sp", dcw[:, :, j], dn_conv_w[j, :].rearrange("(t p) -> p t", p=128))

        mixT = k.sb([128, 16, 1154], BF16)

        def rstd_from_ss(out, ss, n):
            k.act(out, ss, AF.Ln, bias=V(epsc.ap[0:ss.ap.shape[0], :], epsc.deps), scale=1.0 / n)
            k.act(out, out, AF.Exp, scale=-0.5)

        def norm_to_T(xt, np_, gfm, dstT, c0, xnb, ssq, rs):
            k.act(xnb[0:np_, :], xt[0:np_, :], AF.Square, accum=ssq[0:np_, :])
            rstd_from_ss(rs[0:np_, :], ssq[0:np_, :], D)
            k.act(xnb[0:np_, :], xt[0:np_, :], AF.Copy, scale=rs[0:np_, :])
            for hf in range(2):
                pT = PTB[hf]
                for kk in range(8):
                    kt = hf * 8 + kk
                    k.tr(pT[:, kk, 0:np_], xnb[0:np_, kt * 128:(kt + 1) * 128], identb[0:np_, 0:np_])
                gsl = V(gfm.ap[:, hf * 8:(hf + 1) * 8].unsqueeze(2), gfm.deps)
                k.tt("dve", dstT[:, hf * 8:(hf + 1) * 8, c0:c0 + np_], pT[:, 0:8, 0:np_], bc(gsl, [128, 8, np_]), ALU.mult)

        CSP = k.sb([128, 24, 3])
        k.memset("dve", CSP, 0.0)
        eA = ExitStack()
        CST = k.sb([128, 20, 128], F32, eA)
        k.dma("sp", CST, cst[:, :, :])
        ones = CST[:, 1, :]

        class Masks:
            pass
        MS = {}
        for mode, o in (("P", 2), ("S", 10)):
            m = Masks()
            m.U, m.SL, m.BO, m.LAST = (CST[:, o + i, :] for i in range(4))
            m.NEGT_incl, m.NEGT_strict, m.NEG_strict, m.NEG_incl = (CST[:, o + 4 + i, :] for i in range(4))
            MS[mode] = m
        rowmask = CST[:, 18, 0:16]; firstsel = CST[:, 18, 16:32]; seqsel = CST[0:16, 19, :]

        def run_mixer(mode):
            nt = NPT if mode == "P" else 1
            T = nt * 128
            M = MS[mode]
            nlev = 6 if mode == "P" else 2
            with ExitStack() as e1:
                hT = k.sb([128, 16, T], BF16, e1)
                hTt = [k.view(hT, (slice(None), slice(None), slice(t * 128, (t + 1) * 128))) for t in range(nt)]

                def hblk(b, kt):
                    return V(hT.ap[:, kt, b * BW:(b + 1) * BW], [d for t in range(b * tpb, (b + 1) * tpb) for d in hTt[t].deps])
                with ExitStack() as e2:
                    xt2 = [k.sb([128, D], F32, e2) for _ in range(2)]
                    xnb = k.sb([128, D], BF16, e2)
                    ssq = k.sb([128, 1], F32, e2); rs = k.sb([128, 1], F32, e2)
                    for t in range(nt):
                        xt = xt2[t % 2]
                        src = xp[t * 128:(t + 1) * 128, :] if mode == "P" else xs[:, :]
                        k.dma("sp", xt, src)
                        norm_to_T(xt, 128, gmix, hTt[t], 0, xnb, ssq, rs)
                k.fence()
                WG = k.sb([128, 16, 24], BF16, e1)
                k.dma("pool", WG[:, :, 0:16], w_in[:, C_DNB:C_DNB + 16].rearrange("(kt p) c -> p kt c", p=128))
                k.dma("pool", WG[:, :, 16:24], w_in[:, C_MLI:C_MLI + 8].rearrange("(kt p) c -> p kt c", p=128))
                GR = k.sb([128, nt, 24], F32, e1)
                GA_ = k.sb([128, nt, 32], F32, e1)
                for t in range(nt):
                    pg = brange(2, 256, 280)
                    for kt in range(16):
                        k.mm(pg, hTt[t][:, kt, :], WG[:, kt, :], start=(kt == 0), stop=(kt == 15))
                    k.cp("act", GR[:, t, :], pg)
                tmp = k.sb([128, nt, 8], F32, e1)
                b8 = lambda v: bc(V(v.ap.unsqueeze(1), v.deps), [128, nt, v.ap.shape[1]])
                k.act(GA_[:, :, 0:8], GR[:, :, 0:8], AF.Sigmoid)
                k.tt("dve", GA_[:, :, 24:28], GR[:, :, 16:20], b8(ib), ALU.add)
                k.tt("dve", tmp, GR[:, :, 8:16], b8(dtb), ALU.add)
                k.act(tmp, tmp, AF.Exp)
                k.act(tmp, tmp, AF.Ln, bias=onec)
                k.tt("dve", GA_[:, :, 16:24], tmp, b8(negA), ALU.mult)
                k.act(GA_[:, :, 8:16], GA_[:, :, 0:8], AF.Ln)
                k.tt("dve", tmp[:, :, 0:4], GR[:, :, 20:24], b8(fb), ALU.add)
                k.act(tmp[:, :, 0:4], tmp[:, :, 0:4], AF.Exp, scale=-1.0)
                k.act(tmp[:, :, 0:4], tmp[:, :, 0:4], AF.Ln, bias=onec)
                k.ts("dve", GA_[:, :, 28:32], tmp[:, :, 0:4], -1.0, None, ALU.mult)
                GAv = GA_

                NCH = DBG.get('nch', 2) if mode == "P" else 1

                class Chain:
                    pass

                def s128(n=1, dt=F32):
                    return k.sb([128, 128] if n == 1 else [128, n, 128], dt, e1)
                chains = []
                for ci in range(NCH):
                    c = Chain()
                    c.gs = k.sb([128, 8], F32, e1); c.ge = k.sb([128, 4], F32, e1)
                    c.Dg = s128(2); c.E1 = s128(); c.E2 = s128(); c.E3 = s128()
                    c.XX = [s128(2), s128(2)]; c.Pm = [s128(), s128()]; c.qkT = s128()
                    c.kbg = s128(); c.kd = s128(); c.vb = s128(); c.wTn = s128(); c.vn = s128(); c.tA = s128(); c.osb = s128(); c.on = s128()
                    c.mv = s128(); c.sc = k.sb([128, 16], F32, e1); c.ms = k.sb([128, 24], F32, e1)
                    if NCH == 1:
                        B = lambda b, q: bq[b][q]
                        c.Gb = brange(2, 0, 256); c.G2 = B(2, 2); c.GL = B(2, 3)
                        c.KK = B(3, 0); c.QK = B(3, 1); c.KT = B(3, 2); c.VT = B(3, 3)
                        c.X = B(4, 0); c.XT = B(4, 1); c.XXp = brange(4, 0, 256); c.PU = B(4, 2); c.WT = B(4, 3)
                        c.VN = B(5, 0); c.A = B(5, 1); c.Bp = B(5, 2); c.OT = [B(5, 3), B(5, 2)]
                        c.DS = B(6, 0)
                        c.QC = brange(7, 0, 257); c.SV = brange(6, 0, 257); c.VT2 = brange(4, 0, 256)
                    else:
                        b0 = 2 + 3 * ci
                        B = lambda b, q: bq[b0 + b][q]
                        c.Gb = brange(b0, 0, 256); c.G2 = B(0, 2); c.GL = B(0, 3)
                        c.X = B(0, 0); c.XT = B(0, 1); c.XXp = brange(b0, 0, 256); c.PU = B(0, 3)
                        c.KK = B(1, 0); c.QK = B(1, 1); c.KT = B(1, 2); c.VT = B(1, 3)
                        c.WT = B(1, 0); c.VN = B(1, 1)
                        c.A = B(2, 0); c.Bp = B(2, 1); c.DS = B(2, 2); c.OT = [B(2, 3), B(2, 2)]
                        c.QC = brange(b0 + 2, 0, 257); c.SV = brange(b0 + 2, 0, 257); c.VT2 = brange(b0, 0, 256)
                    chains.append(c)
                wbuf = [k.sb([128, 16, 128], BF16, e1) for _ in range(8)]
                wcols = []
                for h in range(8):
                    wcols += [C_DNQ + h * 128, C_DNK + h * 128, C_DNV + h * 128, C_DNZ + h * 128]
                for h in range(4):
                    wcols += [C_MLQ + h * 128, C_MLK + h * 128, C_MLV + h * 256, C_MLV + h * 256 + 128, C_MLO + h * 256, C_MLO + h * 256 + 128]
                wstate = {"loaded": 0}

                def ensure_w(base, n):
                    lim = min(len(wcols), base + 8)
                    while wstate["loaded"] < lim:
                        i_ = wstate["loaded"]
                        k.dma("pool", wbuf[i_ % 8], w_in[:, wcols[i_]:wcols[i_] + 128].rearrange("(kt p) c -> p kt c", p=128))
                        wstate["loaded"] += 1
                    return [wbuf[(base + j) % 8] for j in range(n)]
                if mode == "S":
                    STATE = k.sb([128, 16, 257], F32, e1)
                    Zw = s128(16); Zq = s128(16); KDx = s128(16)
                    k.memset("dve", Zw, 0.0); k.memset("dve", Zq, 0.0)
                    glbc = k.sb([128, 16], F32, e1); fsg = k.sb([128, 16], F32, e1)

                    def zdiag(z):
                        a = z.ap
                        return V(bass.AP(a.tensor, a.offset, [list(a.ap[0]), [136, 16], [1, 8]]), z.deps)

                    def blk8(v):
                        return V(v.ap.rearrange("p (r e) -> p r e", e=8), v.deps)
                BW = 512 if mode == "P" else 128
                nblk = T // BW
                nblk_run = min(nblk, DBG.get('nblk', 99))
                tpb = BW // 128
                NPB = 2 if mode == "P" else 1

                def proj(w, b, dst_ps):
                    for kt in range(16):
                        k.mm(dst_ps, w[:, kt, :], hblk(b, kt), start=(kt == 0), stop=(kt == 15))

                def put_mix(c, e, t, val_ps, gcol, gate):
                    k.stt("dve", c.mv, val_ps, gcol, gate, ALU.mult, ALU.mult)
                    if mode == "S":
                        k.cp("act", mixT[:, e, 1026:1154], c.mv)
                    else:
                        c0 = 2 + (t % 8) * 128
                        dst = mixT[:, e, c0:c0 + 128]
                        if t < 8:
                            k.ts("dve", dst, c.mv, f0, None, ALU.mult)
                        else:
                            k.stt("dve", dst, c.mv, f1, dst, ALU.mult, ALU.add)
                        if t == 7:
                            k.ts("dve", mixT[:, e, 0:2], c.mv[:, 126:128], f1, None, ALU.mult)

                class Sched:
                    def __init__(self):
                        self.active = []
                        self.free = list(chains)
                        self.queue = []

                    def add(self, genf):
                        self.queue.append(genf)

                    def _start(self):
                        while self.queue and self.free:
                            c = self.free.pop(0)
                            self.active.append([self.queue.pop(0)(c), c, 0])

                    def step(self):
                        self._start()
                        for i, ent in enumerate(list(self.active)):
                            g, c, st = ent
                            if st == 1:
                                if any(e2[2] in (0, 1, 2) for e2 in self.active[:self.active.index(ent)]):
                                    continue
                                ent[2] = 2
                            try:
                                r = next(g)
                            except StopIteration:
                                self.active.remove(ent)
                                self.free.append(c)
                                continue
                            if r == "REC":
                                older = self.active[:self.active.index(ent)]
                                ent[2] = 1 if any(e2[2] in (0, 1, 2) for e2 in older) else 2
                            elif r == "END":
                                ent[2] = 3

                    def run(self, leave=0):
                        while self.queue or len(self.active) > leave:
                            self.step()
                sched = Sched()

                with ExitStack() as e3:
                    PB = []
                    for _ in range(NPB):
                        PB.append(dict(qT=k.sb([128, BW], F32, e3), kT=k.sb([128, BW], F32, e3), vT=k.sb([128, BW], F32, e3),
                                       zs=k.sb([128, BW], F32, e3)))
                    if mode == "P":
                        raws = [k.sb([128, 3 + BW], F32, e3) for _ in range(3)]
                    else:
                        raws = [k.sb([128, 16, 11], F32, e3) for _ in range(3)]
                        CSS = k.sb([128, 24, 16, 3], F32, e3)
                        k.memset("dve", CSS, 0.0)
                        dcs_in = k.sb([48, 3072], F32, e3)
                        k.dma("sp", dcs_in, sdc.rearrange("r t c -> (r t) c"))
                        Sout = k.sb([128, 16, 128], F32, e3)
                    cacc = k.sb([128, BW], F32, e3); sq = k.sb([128, BW], F32, e3); rst = k.sb([128, BW], F32, e3)
                    Sst = k.sb([128, 128], F32, e3)

                    def dn_tile(c, h, t, q_, k_, v_, z_, last):
                        g = GAv[:, t, 16 + h:17 + h]; beta = GAv[:, t, h:h + 1]; lnb = GAv[:, t, 8 + h:9 + h]
                        gs, ge = c.gs, c.ge
                        k.mm(c.G2[:, 0:1], M.U, g); k.mm(c.G2[:, 1:2], M.SL, g); k.mm(c.G2[:, 2:3], M.BO, g)
                        yield
                        k.cp("act", gs[:, 0:3], c.G2[:, 0:3])
                        k.act(ge[:, 0:3], c.G2[:, 0:3], AF.Exp)
                        yield
                        Gc, Glast = gs[:, 0:1], gs[:, 2:3]
                        eG, eGrev, glrow = ge[:, 0:1], ge[:, 1:2], ge[:, 2:3]
                        Gp, negGc, bg = gs[:, 3:4], gs[:, 4:5], gs[:, 5:6]
                        k.tt("dve", Gp, Gc, lnb, ALU.add)
                        k.ts("dve", negGc, Gc, -1.0, None, ALU.mult)
                        k.tt("dve", bg, beta, eG, ALU.mult)
                        yield
                        k.ts("dve", c.Dg[:, 0, :], ident, Gc, None, ALU.mult)
                        k.ts("dve", c.Dg[:, 1, :], ident, Gp, None, ALU.mult)
                        yield
                        Gb = c.Gb
                        k.mm(Gb, ones, V(c.Dg.ap.rearrange("p a b -> p (a b)"), c.Dg.deps))
                        yield
                        k.tt("dve", c.E1, Gb[:, 0:128], M.NEGT_incl, ALU.add)
                        k.tt("dve", c.E2, Gb[:, 128:256], M.NEGT_strict, ALU.add)
                        k.stt("dve", c.E3, Gb[:, 0:128], -1.0, M.NEG_strict, ALU.mult, ALU.add)
                        yield
                        k.act(c.E1, c.E1, AF.Exp, bias=negGc)
                        k.act(c.E2, c.E2, AF.Exp, bias=negGc)
                        k.act(c.E3, c.E3, AF.Exp, bias=Gp)
                        yield
                        k.mm(c.KK, k_, k_); k.mm(c.QK, k_, q_)
                        yield
                        XX, Pm = c.XX, c.Pm
                        cur = 0
                        k.stt("dve", XX[0][:, 0, :], c.KK, -1.0, c.E2, ALU.mult, ALU.mult)
                        k.stt("dve", XX[0][:, 1, :], c.KK, -1.0, c.E3, ALU.mult, ALU.mult)
                        k.tt("dve", c.qkT, c.QK, c.E1, ALU.mult)
                        pc = 0
                        k.tt("dve", Pm[0], XX[0][:, 0, :], ident, ALU.add)
                        yield
                        k.tr(c.KT, k_, ident); k.tr(c.VT, v_, ident)
                        yield
                        k.ts("dve", c.kbg, c.KT, bg, None, ALU.mult)
                        k.ts("dve", c.kd, c.KT, eGrev, None, ALU.mult)
                        k.act(c.vb, c.VT, AF.Copy, scale=beta)
                        yield
                        for lv in range(nlev):
                            X_, XT_ = XX[cur][:, 0, :], XX[cur][:, 1, :]
                            k.mm(c.X, XT_, X_); k.mm(c.XT, X_, XT_)
                            yield
                            nx = 1 - cur
                            k.cp("act", V(XX[nx].ap.rearrange("p a b -> p (a b)"), XX[nx].deps), c.XXp)
                            yield
                            k.mm(c.PU, XX[nx][:, 1, :], Pm[pc])
                            yield
                            k.tt("dve", Pm[1 - pc], Pm[pc], c.PU, ALU.add)
                            yield
                            cur = nx; pc = 1 - pc
                        PT = Pm[pc]
                        k.mm(c.WT, c.kbg, PT)
                        yield
                        k.act(c.wTn, c.WT, AF.Copy, scale=-1.0)
                        yield
                        yield "REC"
                        vnp, Ap, Bp = c.VN, c.A, c.Bp
                        if mode == "P":
                            k.mm(vnp, PT, c.vb, start=True, stop=False)
                            k.mm(vnp, c.wTn, Sst, start=False, stop=True)
                            k.mm(Ap, q_, Sst)
                            yield
                            k.cp("dve", c.vn, vnp)
                            yield
                            k.mm(c.DS, c.kd, c.vn)
                            k.mm(Bp, c.qkT, c.vn)
                            yield
                            k.stt("dve", Sst, Sst, glrow, c.DS, ALU.mult, ALU.add)
                            if last:
                                k.dma("sp", o_pS[h, :, :], Sst)
                            yield "END"
                        else:
                            k.cp("act", zdiag(Zw), blk8(c.wTn))
                            k.cp("act", zdiag(Zq), blk8(q_))
                            k.tt("dve", KDx, bc(V(c.kd.ap.unsqueeze(1), c.kd.deps), [128, 16, 128]),
                                 bc(V(rowmask.ap.unsqueeze(2), rowmask.deps), [128, 16, 128]), ALU.mult)
                            k.mm(vnp, PT, c.vb, start=True, stop=False)
                            for r in range(16):
                                k.mm(vnp, Zw[:, r, :], STATE[:, r, 0:128], start=False, stop=(r == 15))
                            for r in range(16):
                                k.mm(Ap, Zq[:, r, :], STATE[:, r, 0:128], start=(r == 0), stop=(r == 15))
                            k.cp("dve", c.vn, vnp)
                            k.mm(Bp, c.qkT, c.vn)
                            k.ts("dve", fsg, firstsel, Glast, None, ALU.mult)
                            k.mm(c.GL[:, 0:16], ones, fsg)
                            k.act(glbc, c.GL[:, 0:16], AF.Exp)
                            for r in range(16):
                                dp = bq[6 + (r // 4) % 2][r % 4]
                                k.mm(dp, KDx[:, r, :], c.vn)
                                k.stt("dve", Sout[:, r, :], STATE[:, r, 0:128], glbc[:, r:r + 1], dp, ALU.mult, ALU.add)
                            k.dma("sp", o_sS[:, h, :, :].rearrange("r k v -> k r v"), Sout)
                            yield "END"
                        k.act(c.tA, Ap, AF.Copy, scale=eG)
                        yield
                        k.tt("dve", c.osb, c.tA, Bp, ALU.add)
                        yield
                        k.act(c.on, c.osb, AF.Square, accum=c.sc[:, 0:1])
                        rstd_from_ss(c.sc[:, 1:2], c.sc[:, 0:1], 128)
                        yield
                        k.act(c.on, c.osb, AF.Copy, scale=c.sc[:, 1:2])
                        yield
                        k.tr(c.OT[0], c.on, ident)
                        yield
                        put_mix(c, h, t, c.OT[0], gdn[:, 0:1], z_)
                        yield

                    for h in range(DBG.get('dn', 8)):
                        ws = ensure_w(4 * h, 4)
                        if mode == "P":
                            k.memset("dve", Sst, 0.0)
                        else:
                            k.dma("sp", STATE[:, :, 0:128], sS[:, h, :, :].rearrange("r k v -> k r v"))
                        for b in range(nblk_run):
                            pb = PB[b % NPB]
                            qT, kT, vT, zs = pb["qT"], pb["kT"], pb["vT"], pb["zs"]
                            for wi, dst in enumerate((qT, kT, vT)):
                                acc = brange(wi % 2, 0, BW)
                                proj(ws[wi], b, acc)
                                ct = wi * 8 + h
                                raw = raws[wi]
                                if mode == "P":
                                    if b == 0:
                                        k.memset("dve", raw[:, 0:3], 0.0)
                                    else:
                                        k.cp("dve", raw[:, 0:3], raw[:, BW:BW + 3])
                                    k.cp("act", raw[:, 3:3 + BW], acc)
                                    if b == nblk - 1:
                                        k.cp("dve", CSP[:, ct, :], raw[:, BW:BW + 3])
                                    sh = lambda j, raw=raw: raw[:, j:j + BW]
                                    ca = cacc
                                else:
                                    pt = V(bq[3][0].ap[:, 0:48], bq[3][0].deps)
                                    k.tr(pt, dcs_in[:, ct * 128:(ct + 1) * 128], ident[0:48, 0:48])
                                    k.cp("dve", raw[:, :, 0:3], V(pt.ap.rearrange("p (r t) -> p r t", t=3), pt.deps))
                                    k.cp("act", raw[:, :, 3:11], V(acc.ap.rearrange("p (r e) -> p r e", e=8), acc.deps))
                                    k.cp("dve", CSS[:, ct, :, :], raw[:, :, 8:11])
                                    sh = lambda j, raw=raw: raw[:, :, j:j + 8]
                                    ca = V(cacc.ap.rearrange("p (r e) -> p r e", e=8), cacc.deps)
                                k.ts("dve", ca, sh(0), dcw[:, ct, 0:1], None, ALU.mult)
                                for j in (1, 2, 3):
                                    k.stt("dve", ca, sh(j), dcw[:, ct, j:j + 1], ca, ALU.mult, ALU.add)
                                if wi == 2:
                                    k.act(vT, cacc, AF.Silu)
                                else:
                                    k.act(cacc, cacc, AF.Silu)
                                    k.tt("dve", sq, cacc, cacc, ALU.mult)
                                    pss = brange(wi % 2, 0, BW)
                                    k.mm(pss, ones, sq)
                                    k.act(rst, pss, AF.Ln, bias=epsc)
                                    k.act(rst, rst, AF.Exp, scale=-0.5)
                                    if wi == 0:
                                        k.stt("dve", dst, cacc, 128.0 ** -0.5, rst, ALU.mult, ALU.mult)
                                    else:
                                        k.tt("dve", dst, cacc, rst, ALU.mult)
                            acc = brange(1, 0, BW)
                            proj(ws[3], b, acc)
                            k.act(zs, acc, AF.Silu)
                            for tt_ in range(tpb):
                                t = b * tpb + tt_
                                sl = slice(tt_ * 128, (tt_ + 1) * 128)
                                last = (mode == "P" and t == nt - 1)
                                sched.add(lambda c, h=h, t=t, a=(qT[:, sl], kT[:, sl], vT[:, sl], zs[:, sl]), last=last:
                                          dn_tile(c, h, t, a[0], a[1], a[2], a[3], last))
                            sched.run(leave=1 if b < nblk_run - 1 else 0)
                    if DBG.get('noconvout'):
                        pass
                    elif mode == "P":
                        stg = k.sb([24, 3, 128], F32, e3)
                        for j in range(3):
                            pt = V(bq[3][0].ap[0:24, :], bq[3][0].deps)
                            k.op("pe", lambda: nc.tensor.matmul(pt.ap, lhsT=CSP.ap[:, :, j], rhs=ident.ap, start=True, stop=True), [pt], [CSP, ident])
                            k.cp("dve", stg[:, j, :], pt)
                        k.dma("sp", o_pdc.rearrange("j (t p) -> t j p", p=128), stg)
                    else:
                        stg = k.sb([48, 24, 128], F32, e3)
                        for ct in range(24):
                            pt = V(bq[3][ct % 2].ap[0:48, :], bq[3][ct % 2].deps)
                            k.tr(pt, V(CSS.ap[:, ct, :, :].rearrange("p r t -> p (r t)"), CSS.deps), ident)
                            k.cp("dve", stg[:, ct, :], pt)
                        k.dma("sp", o_sdc.rearrange("r t (c p) -> (r t) c p", p=128), stg)

                k.fence()
                with ExitStack() as e3:
                    PB = []
                    NPBM = 1
                    for _ in range(NPBM):
                        PB.append(dict(qT=k.sb([128, BW], F32, e3), kT=k.sb([128, BW], F32, e3),
                                       vT=k.sb([128, 2, BW], F32, e3), sg=k.sb([128, 2, BW], F32, e3)))
                    Cst = k.sb([128, 257], F32, e3)
                    mrow = k.sb([128, 1], F32, e3)
                    for c in chains:
                        c.vext = k.sb([128, 257], F32, e3)
                        k.memset("dve", c.vext[:, 256:257], 1.0)
                        c.Dm, c.Em, c.ssb, c.sT, c.kw = (k.sb([128, 128], F32, e3) for _ in range(5))
                        c.hsb = k.sb([128, 256], F32, e3); c.hn = k.sb([128, 256], F32, e3); c.tq = k.sb([128, 257], F32, e3)
                    if mode == "S":
                        m0 = k.sb([16, 4], F32, e3); k.dma("sp", m0, sm[:, :])
                        mout = k.sb([16, 4], F32, e3)
                        k.memset("dve", mout, 0.0)
                        KWx = KDx

                    def ml_tile(c, h, t, q_, k_, v0_, v1_, sg0_, sg1_, last):
                        ms = c.ms
                        ig = GAv[:, t, 24 + h:25 + h]; lf = GAv[:, t, 28 + h:29 + h]
                        G2 = c.G2
                        k.mm(G2[:, 0:1], M.U, lf); k.mm(G2[:, 1:2], M.SL, lf); k.mm(G2[:, 2:3], M.BO, lf)
                        yield
                        k.cp("act", ms[:, 0:3], G2[:, 0:3])
                        yield
                        bcs, brev, btot = ms[:, 0:1], ms[:, 1:2], ms[:, 2:3]
                        a_ = ms[:, 3:4]; bm = ms[:, 4:5]; rmax = ms[:, 5:6]; mnew = ms[:, 6:7]; negm = ms[:, 7:8]
                        inter = ms[:, 8:9]; mend = ms[:, 9:10]; wend = ms[:, 10:11]; carry = ms[:, 11:12]
                        den = ms[:, 12:13]; emn = ms[:, 13:14]; rd = ms[:, 14:15]; ir = ms[:, 15:16]; ssq_ = ms[:, 16:17]; rs_ = ms[:, 17:18]
                        tmpc = ms[:, 18:19]; mprev = ms[:, 19:20]
                        k.tt("dve", a_, ig, bcs, ALU.subtract)
                        k.ts("dve", c.Dg[:, 0, :], ident, a_, None, ALU.mult)
                        yield
                        Abc = c.Gb[:, 0:128]
                        k.mm(Abc, ones, c.Dg[:, 0, :])
                        yield
                        k.stt("dve", c.Dm, Abc, bcs, M.NEG_incl, ALU.add, ALU.add)
                        k.redmax(rmax, c.Dm)
                        yield
                        k.mm(c.KK, q_, k_)
                        k.tr(c.KT, k_, ident)
                        k.tr(c.VT2[:, 0:128], v0_, ident); k.tr(c.VT2[:, 128:256], v1_, ident)
                        yield
                        k.cp("act", c.vext[:, 0:256], c.VT2)
                        yield
                        yield "REC"
                        if mode == "S":
                            k.mm(G2[:, 4:5], seqsel, m0[:, h:h + 1])
                            k.cp("act", mprev, G2[:, 4:5])
                        else:
                            k.cp("dve", mprev, mrow)
                        k.tt("dve", bm, bcs, mprev, ALU.add)
                        k.tt("dve", mnew, bm, rmax, ALU.max)
                        k.mm(G2[:, 8:9], M.LAST, mnew)
                        if mode == "P":
                            k.cp("dve", mrow, G2[:, 8:9])
                            QC = c.QC
                            k.mm(QC, q_, Cst)
                            yield
                            k.cp("act", c.tq, QC)
                            yield "END" if False else None
                        else:
                            QC = c.QC
                            k.cp("act", zdiag(Zq), blk8(q_))
                            for r in range(16):
                                k.mm(QC, Zq[:, r, :], STATE[:, r, :], start=(r == 0), stop=(r == 15))
                            k.cp("act", c.tq, QC)
                        k.cp("act", mend, G2[:, 8:9])
                        k.ts("dve", negm, mnew, -1.0, None, ALU.mult)
                        k.tt("dve", tmpc, bm, mnew, ALU.subtract)
                        yield
                        k.act(inter, tmpc, AF.Exp)
                        k.act(c.Em, c.Dm, AF.Exp, bias=negm)
                        yield
                        k.tt("dve", c.ssb, c.KK, c.Em, ALU.mult)
                        yield
                        k.tr(c.QK, c.ssb, ident)
                        yield
                        k.cp("act", c.sT, c.QK)
                        k.tt("dve", tmpc, brev, ig, ALU.add)
                        k.tt("dve", tmpc, tmpc, mend, ALU.subtract)
                        yield
                        k.act(wend, tmpc, AF.Exp)
                        k.tt("dve", carry, btot, mprev, ALU.add)
                        k.tt("dve", carry, carry, mend, ALU.subtract)
                        yield
                        k.ts("dve", c.kw, c.KT, wend, None, ALU.mult)
                        SV = c.SV
                        k.mm(SV, c.sT, c.vext)
                        yield
                        k.tt("dve", den, c.tq[:, 256:257], inter, ALU.mult)
                        k.tt("dve", den, den, SV[:, 256:257], ALU.add)
                        k.act(den, den, AF.Abs)
                        k.act(emn, negm, AF.Exp)
                        yield
                        k.tt("dve", den, den, emn, ALU.max)
                        k.recip(rd, den)
                        k.tt("dve", ir, inter, rd, ALU.mult)
                        yield
                        k.ts("dve", c.tq[:, 0:256], c.tq[:, 0:256], ir, None, ALU.mult)
                        k.stt("dve", c.hsb, SV[:, 0:256], rd, c.tq[:, 0:256], ALU.mult, ALU.add)
                        yield
                        if mode == "P":
                            k.act(carry, carry, AF.Exp)
                            k.mm(SV, c.kw, c.vext)
                            yield
                            k.stt("dve", Cst, Cst, carry, SV, ALU.mult, ALU.add)
                            if last:
                                k.dma("sp", o_pC[h, :, :], Cst[:, 0:256])
                                k.dma("sp", o_pn[h, :].rearrange("(k o) -> k o", o=1), Cst[:, 256:257])
                                k.dma("sp", o_pm[0:1, h:h + 1], mrow[0:1, 0:1])
                        else:
                            k.tt("dve", KWx, bc(V(c.kw.ap.unsqueeze(1), c.kw.deps), [128, 16, 128]),
                                 bc(V(rowmask.ap.unsqueeze(2), rowmask.deps), [128, 16, 128]), ALU.mult)
                            k.ts("dve", fsg, firstsel, carry, None, ALU.mult)
                            k.mm(c.GL[:, 0:16], ones, fsg)
                            k.act(glbc, c.GL[:, 0:16], AF.Exp)
                            for r in range(16):
                                dp = brange(6 + r % 2, 0, 257)
                                k.mm(dp, KWx[:, r, :], c.vext)
                                k.stt("dve", STATE[:, r, :], STATE[:, r, :], glbc[:, r:r + 1], dp, ALU.mult, ALU.add)
                            k.dma("sp", o_sC[:, h, :, :].rearrange("r k v -> k r v"), STATE[:, :, 0:256])
                            k.dma("sp", o_sn[:, h, :].rearrange("r k -> k r"), STATE[:, :, 256])
                            pm_ = V(G2.ap[0:16, 12:13], G2.deps)
                            k.mm(pm_, firstsel, mend)
                            k.cp("dve", mout[:, h:h + 1], pm_)
                        yield "END"
                        k.act(c.hn, c.hsb, AF.Square, accum=ssq_)
                        rstd_from_ss(rs_, ssq_, 256)
                        yield
                        k.act(c.hn, c.hsb, AF.Copy, scale=rs_)
                        yield
                        for d2, sg_ in enumerate((sg0_, sg1_)):
                            k.tr(c.OT[d2], c.hn[:, d2 * 128:(d2 + 1) * 128], ident)
                            yield
                            put_mix(c, 8 + 2 * h + d2, t, c.OT[d2], gml[:, 2 * h + d2:2 * h + d2 + 1], sg_)
                            yield

                    for h in range(DBG.get('ml', 4)):
                        ws = ensure_w(32 + 6 * h, 6)
                        if mode == "P":
                            k.memset("dve", Cst, 0.0); k.memset("dve", mrow, 0.0)
                        else:
                            k.dma("sp", STATE[:, :, 0:256], sC[:, h, :, :].rearrange("r k v -> k r v"))
                            k.dma("sp", STATE[:, :, 256], sn[:, h, :].rearrange("r k -> k r"))
                        for b in range(nblk_run):
                            pb = PB[b % NPBM]
                            qT, kT, vT, sg = pb["qT"], pb["kT"], pb["vT"], pb["sg"]
                            acc = brange(0, 0, BW); proj(ws[0], b, acc)
                            k.act(qT, acc, AF.Copy, scale=128.0 ** -0.5)
                            acc = brange(1, 0, BW); proj(ws[1], b, acc)
                            k.cp("dve", kT, acc)
                            for d2 in range(2):
                                acc = brange(0, 0, BW); proj(ws[2 + d2], b, acc)
                                k.cp("act", vT[:, d2, :], acc)
                                acc = brange(1, 0, BW); proj(ws[4 + d2], b, acc)
                                k.act(sg[:, d2, :], acc, AF.Sigmoid)
                            for tt_ in range(tpb):
                                t = b * tpb + tt_
                                sl = slice(tt_ * 128, (tt_ + 1) * 128)
                                last = (mode == "P" and t == nt - 1)
                                sched.add(lambda c, h=h, t=t, a=(qT[:, sl], kT[:, sl], vT[:, 0, sl], vT[:, 1, sl], sg[:, 0, sl], sg[:, 1, sl]), last=last:
                                          ml_tile(c, h, t, a[0], a[1], a[2], a[3], a[4], a[5], last))
                            sched.run(leave=0)
                    if mode == "S":
                        k.dma("sp", o_sm[:, :], mout)
                k.fence()

        for md in DBG.get("modes", ["P", "S"]):
            run_mixer(md)
            k.fence()
        eA.close()
        k.fence()

        with ExitStack() as eb:
            X1 = [k.sb([128, D], F32, eb) for _ in range(3)]
            xtmp = k.sb([128, D], F32, eb)
            h2T = k.sb([128, 16, 386], BF16, eb)
            actT = k.sb([128, 44, 386], BF16, eb)
            Wo = [k.sb([128, 16, 256], BF16, eb) for _ in range(2)]
            Wu = [k.sb([128, 16, 128], BF16, eb) for _ in range(4)]
            Wd = [k.sb([128, 44, 128], BF16, eb) for _ in range(2)]
            gfin = k.sb([128, D], F32, eb)
            k.dma("sp", gfin, norm_final_g.partition_broadcast(128))
            fcw = k.sb([128, 88, 3], F32, eb); fcb = k.sb([128, 88], F32, eb)
            for j in range(3):
                k.dma("sp", fcw[:, :, j], ffn_conv_w[j, :].rearrange("(t p) -> p t", p=128))
            k.dma("sp", fcb, ffn_conv_b.rearrange("(t p) -> p t", p=128))
            CF = k.sb([128, 88, 2], F32, eb)
            Rb = [k.sb([128, 388], F32, eb) for _ in range(2)]
            Cb = [k.sb([128, 386], F32, eb) for _ in range(2)]
            RS = [k.sb([128, 16, 10], F32, eb) for _ in range(2)]
            so32 = k.sb([128, 32], F32, eb)
            stgs = [k.sb([32, 128], F32, eb) for _ in range(2)]
            PFO = k.sb([128, 88, 2], F32, eb)
            xnb = k.sb([128, D], BF16, eb)
            ssq = k.sb([128, 1], F32, eb); rs = k.sb([128, 1], F32, eb)
            yT = k.sb([128, 384], F32, eb)
            sfi = [k.sb([32, 128], F32, eb) for _ in range(2)]
            stg = k.sb([88, 2, 128], F32, eb)
            wctr = {"o": 0, "u": 0, "d": 0}

            blocks = [
                (0, 386, 2, [(2, xh[0:2, :], 0, None)] + [(128, xo[j * 128:(j + 1) * 128, :], 2 + j * 128, j * 128) for j in range(3)]),
                (386, 384, 0, [(128, xo[j * 128:(j + 1) * 128, :], (j - 3) * 128, j * 128) for j in (3, 4, 5)]),
                (770, 384, 0, [(128, xo[j * 128:(j + 1) * 128, :], (j - 6) * 128, j * 128) for j in (6, 7, 8)]),
            ]
            for bi, (c0, Wb, cs, tiles) in enumerate(blocks[:DBG.get('B', 3)]):
                tbuf = []
                xi = 0
                for (np_, src, col, orow) in tiles:
                    if orow is None:
                        xt = xtmp
                    else:
                        xt = X1[xi]; xi += 1
                    k.dma("sp", xt[0:np_, :], src)
                    tbuf.append(xt)
                for db in range(8):
                    wo = Wo[wctr["o"] % 2]; wctr["o"] += 1
                    k.dma("pool", wo, w_out[:, db * 256:(db + 1) * 256].rearrange("(e p) c -> p e c", p=128))
                    for ti, (np_, src, col, orow) in enumerate(tiles):
                        ps = V(banks[2 + ti % 2][0:np_, 0:256], [bdep[2 + ti % 2]])
                        for e in range(16):
                            k.mm(ps, mixT[:, e, c0 + col:c0 + col + np_], wo[:, e, :], start=(e == 0), stop=(e == 15))
                        xs_ = tbuf[ti][0:np_, db * 256:(db + 1) * 256]
                        k.tt("dve", xs_, xs_, ps, ALU.add)
                for ti, (np_, src, col, orow) in enumerate(tiles):
                    norm_to_T(tbuf[ti], np_, gffn, h2T, col, xnb, ssq, rs)
                for ft in range(44):
                    accs = []
                    for which in range(2):
                        wu = Wu[wctr["u"] % 4]; wctr["u"] += 1
                        cb0 = which * DFF + ft * 128
                        k.dma("pool", wu, w_up[:, cb0:cb0 + 128].rearrange("(kt p) c -> p kt c", p=128))
                        acc = brange(which, 0, Wb)
                        for kt in range(16):
                            k.mm(acc, wu[:, kt, :], h2T[:, kt, 0:Wb], start=(kt == 0), stop=(kt == 15))
                        accs.append(acc)
                    for which in range(2):
                        ct = which * 44 + ft
                        acc = accs[which]; R = Rb[which]; C = Cb[which]
                        w0, w1, w2, bb = fcw[:, ct, 0:1], fcw[:, ct, 1:2], fcw[:, ct, 2:3], fcb[:, ct:ct + 1]
                        if bi == 0:
                            k.cp("act", R[:, 0:386], acc)
                            L = 384
                            k.cp("act", CF[:, ct, :], R[:, 384:386])
                        elif bi == 1:
                            k.cp("dve", R[:, 0:2], CF[:, ct, :])
                            k.cp("act", R[:, 2:386], acc)
                            L = 384
                            k.cp("act", CF[:, ct, :], R[:, 384:386])
                        else:
                            k.cp("dve", R[:, 0:2], CF[:, ct, :])
                            k.cp("act", R[:, 2:258], acc[:, 0:256])
                            L = 256
                            k.cp("act", PFO[:, ct, :], R[:, 256:258])
                            sf = sfi[ct % 2]
                            k.dma("sp", sf, sfc.rearrange("r t c -> (r t) c")[:, ct * 128:(ct + 1) * 128])
                            pt = V(banks[3][:, 0:32], bq[3][0].deps)
                            k.tr(pt, sf, ident[0:32, 0:32])
                            rs_ = RS[which]
                            k.cp("dve", rs_[:, :, 0:2], V(pt.ap.rearrange("p (r t) -> p r t", t=2), pt.deps))
                            k.cp("act", rs_[:, :, 2:10], V(acc.ap[:, 256:384].rearrange("p (r e) -> p r e", e=8), acc.deps))
                            k.cp("act", V(so32.ap.rearrange("p (r t) -> p r t", t=2), so32.deps), rs_[:, :, 8:10])
                            pt2 = V(banks[3][0:32, 128:256], bq[3][1].deps)
                            k.tr(pt2, so32, ident)
                            k.cp("dve", stgs[ct % 2], pt2)
                            k.dma("sp", o_sfc.rearrange("r t c -> (r t) c")[:, ct * 128:(ct + 1) * 128], stgs[ct % 2])
                            c3 = V(C.ap[:, 256:384].rearrange("p (r e) -> p r e", e=8), C.deps)
                            k.ts("dve", c3, rs_[:, :, 0:8], w0, bb, ALU.mult, ALU.add)
                            k.stt("dve", c3, rs_[:, :, 1:9], w1, c3, ALU.mult, ALU.add)
                            k.stt("dve", c3, rs_[:, :, 2:10], w2, c3, ALU.mult, ALU.add)
                        k.ts("dve", C[:, 0:L], R[:, 0:L], w0, bb, ALU.mult, ALU.add)
                        k.stt("dve", C[:, 0:L], R[:, 1:L + 1], w1, C[:, 0:L], ALU.mult, ALU.add)
                        k.stt("dve", C[:, 0:L], R[:, 2:L + 2], w2, C[:, 0:L], ALU.mult, ALU.add)
                    k.act(Cb[0][:, 0:384], Cb[0][:, 0:384], AF.Silu)
                    k.tt("dve", actT[:, ft, cs:cs + 384], Cb[0][:, 0:384], Cb[1][:, 0:384], ALU.mult)
                otiles = [(tb, col, orow) for tb, (np_, src, col, orow) in zip(tbuf, tiles) if orow is not None]
                for dt_ in range(16):
                    wd = Wd[wctr["d"] % 2]; wctr["d"] += 1
                    k.dma("pool", wd, w_down[:, dt_ * 128:(dt_ + 1) * 128].rearrange("(f p) c -> p f c", p=128))
                    acc = brange(dt_ % 2, 0, 384)
                    for ft in range(44):
                        k.mm(acc, wd[:, ft, :], actT[:, ft, cs:cs + 384], start=(ft == 0), stop=(ft == 43))
                    k.cp("act", yT, acc)
                    for j, (tb, col, orow) in enumerate(otiles):
                        pt = bq[4 + j % 2][j // 2]
                        k.tr(pt, yT[:, j * 128:(j + 1) * 128], ident)
                        xs_ = tb[:, dt_ * 128:(dt_ + 1) * 128]
                        k.tt("dve", xs_, xs_, pt, ALU.add)
                for (tb, col, orow) in otiles:
                    k.act(xnb, tb, AF.Square, accum=ssq)
                    rstd_from_ss(rs, ssq, D)
                    k.stt("dve", tb, tb, rs, gfin, ALU.mult, ALU.mult)
                    k.dma("sp", yo[orow:orow + 128, :], tb)
            for t2 in range(2 if DBG.get('B', 3) == 3 else 0):
                pt = V(banks[3][0:88, 0:128], bq[3][0].deps)
                k.op("pe", lambda: nc.tensor.matmul(pt.ap, lhsT=PFO.ap[:, :, t2], rhs=ident.ap, start=True, stop=True), [pt], [PFO, ident])
                k.cp("dve", stg[:, t2, :], pt)
            if DBG.get('B', 3) == 3:
                k.dma("sp", o_pfc.rearrange("t (c p) -> c t p", p=128), stg[:, 0:2, :])
        k.finish()
    return nc


_CACHE = {}


def _consts():
    c = np.zeros((128, 20, 128), np.float32)
    idx = np.arange(128)
    c[:, 0, :] = np.eye(128)
    c[:, 1, :] = 1.0
    for o, bs in ((2, 128), (10, 8)):
        blk = idx // bs
        same = blk[:, None] == blk[None, :]
        kk = idx[:, None]; ii = idx[None, :]
        c[:, o + 0, :] = same & (kk <= ii)
        c[:, o + 1, :] = same & (kk > ii)
        c[:, o + 2, :] = same
        c[:, o + 3, :] = (kk == (blk[None, :] * bs + bs - 1))
        c[:, o + 4, :] = np.where(same & (kk <= ii), 0.0, NEG)
        c[:, o + 5, :] = np.where(same & (kk < ii), 0.0, NEG)
        c[:, o + 6, :] = np.where(same & (ii < kk), 0.0, NEG)
        c[:, o + 7, :] = np.where(same & (ii <= kk), 0.0, NEG)
    seq = idx // 8
    c[:, 18, 0:16] = seq[:, None] == np.arange(16)[None, :]
    c[:, 18, 16:32] = (idx[:, None] == (np.arange(16)[None, :] * 8))
    c[0:16, 19, :] = np.arange(16)[:, None] == seq[None, :]
    return c


def kernel(x_prompt, x_sample, state_dn_conv, state_dn_S, state_ml_C, state_ml_n, state_ml_m,
           state_ffn_conv, norm_mix_g, w_in, dn_conv_w, dn_A_log, dn_dt_bias, dn_norm_g,
           ml_i_bias, ml_f_bias, ml_norm_g, w_out, norm_ffn_g, w_up, ffn_conv_w, ffn_conv_b,
           w_down, norm_final_g):
    if "nc" not in _CACHE:
        _CACHE["nc"] = build_program()
    nc = _CACHE["nc"]
    in_maps = make_in_maps(**{k_: v_ for k_, v_ in locals().items() if k_ != "nc"})
    res = run_bass_kernel_spmd(nc, in_maps, core_ids=list(range(8))).results
    return assemble(res)


def make_in_maps(x_prompt, x_sample, state_dn_conv, state_dn_S, state_ml_C, state_ml_n, state_ml_m,
                 state_ffn_conv, norm_mix_g, w_in, dn_conv_w, dn_A_log, dn_dt_bias, dn_norm_g,
                 ml_i_bias, ml_f_bias, ml_norm_g, w_out, norm_ffn_g, w_up, ffn_conv_w, ffn_conv_b,
                 w_down, norm_final_g):
    f = lambda a: np.ascontiguousarray(np.asarray(a, dtype=np.float32))
    x_prompt = f(x_prompt); x_sample = f(x_sample)
    cst = _consts()
    shared = dict(
        cst=cst, norm_mix_g=f(norm_mix_g)[0], w_in=f(w_in)[0], dn_conv_w=f(dn_conv_w)[0], dn_A_log=f(dn_A_log)[0],
        dn_dt_bias=f(dn_dt_bias)[0], dn_norm_g=f(dn_norm_g)[0], ml_i_bias=f(ml_i_bias)[0], ml_f_bias=f(ml_f_bias)[0],
        ml_norm_g=f(ml_norm_g)[0], w_out=f(w_out)[0], norm_ffn_g=f(norm_ffn_g)[0], w_up=f(w_up)[0],
        ffn_conv_w=f(ffn_conv_w)[0], ffn_conv_b=f(ffn_conv_b)[0], w_down=f(w_down)[0], norm_final_g=f(norm_final_g))
    sdc = f(state_dn_conv)[0]; sS = f(state_dn_S)[0]; sC = f(state_ml_C)[0]; sn = f(state_ml_n)[0]
    sm = f(state_ml_m)[0]; sfc = f(state_ffn_conv)[0]
    in_maps = []
    for c in range(8):
        s, hf = c // 2, c % 2
        r0, r1 = 16 * c, 16 * c + 16
        xsamp = x_sample[r0:r1].reshape(128, D)
        xo = np.concatenate([x_prompt[s, hf * 1024:(hf + 1) * 1024], xsamp], axis=0)
        xh = x_prompt[s, 1022:1024].copy() if hf == 1 else np.zeros((2, D), np.float32)
        selv = np.zeros((128, 2), np.float32); selv[:, hf] = 1.0
        m = dict(shared)
        m.update(xp=x_prompt[s], xs=xsamp, xo=np.ascontiguousarray(xo), xh=xh, sel=selv,
                 sdc=sdc[r0:r1], sS=sS[r0:r1], sC=sC[r0:r1], sn=sn[r0:r1], sm=sm[r0:r1], sfc=sfc[r0:r1])
        in_maps.append({kk: np.ascontiguousarray(v) for kk, v in m.items()})
    return in_maps


def assemble(res):
    y_prompt = np.zeros((4, 2048, D), np.float32); y_sample = np.zeros((128, 8, D), np.float32)
    for c in range(8):
        s, hf = c // 2, c % 2
        y_prompt[s, hf * 1024:(hf + 1) * 1024] = res[c]["yo"][0:1024]
        y_sample[16 * c:16 * c + 16] = res[c]["yo"][1024:1152].reshape(16, 8, D)
    odd = [res[2 * s + 1] for s in range(4)]
    st = lambda name: np.stack([o[name] for o in odd], 0)[None]
    p_dn_conv = st("o_pdc"); p_dn_S = st("o_pS"); p_ml_C = st("o_pC"); p_ml_n = st("o_pn")
    p_ml_m = np.stack([o["o_pm"][0] for o in odd], 0)[None]
    p_ffn = st("o_pfc")
    cat = lambda name: np.concatenate([res[c][name] for c in range(8)], 0)[None]
    return (y_prompt, y_sample, p_dn_conv, p_dn_S, p_ml_C, p_ml_n, p_ml_m, p_ffn,
            cat("o_sdc"), cat("o_sS"), cat("o_sC"), cat("o_sn"), cat("o_sm"), cat("o_sfc"))
```
